# Optimizing a Trainium2 kernel written in Bass

```python
import jax, jax.numpy as jnp
from jax import lax
import numpy as np

D_MODEL = 1024
BATCH = 16
SEQ = 256
DEPTH = 4
DEC_BATCH = 4
DEC_SEQ = 2048
PAST_LEN = 512

GRID_W = 64
N_MIXERS = 2
N_CONV_LAYERS = (DEPTH + 1) // 2
N_NA_LAYERS = DEPTH // 2
CONV_WIDTH = D_MODEL
CONV_K = 31
N_HEADS = 16
HEAD_DIM = D_MODEL // N_HEADS
NA_WIDTH = N_HEADS * HEAD_DIM
NA_ROW_WIN = 8
NA_COL_WIN = 16
Q_BLOCK = 128
EPS = 1e-6
NEG_INF = -1e30

kernel_name = 'hybrid_conv_natten_prefix_dit_step'


def rms_norm(x, g):
    x32 = x.astype(jnp.float32)
    y = x32 * lax.rsqrt(jnp.mean(x32 * x32, axis=-1, keepdims=True) + EPS)
    return y.astype(x.dtype) * g


def layer_norm(x, g, b):
    x32 = x.astype(jnp.float32)
    mu = jnp.mean(x32, axis=-1, keepdims=True)
    var = jnp.mean(jnp.square(x32 - mu), axis=-1, keepdims=True)
    return ((x32 - mu) * lax.rsqrt(var + EPS)).astype(x.dtype) * g + b


def ada_params(cvec, w, b):
    m = jax.nn.silu(cvec) @ w + b
    return jnp.split(m, 3, axis=-1)


def modulate(x, g, shift, scale):
    return rms_norm(x, g) * (1 + scale[:, None]) + shift[:, None]


def conv_branch(h, w_in, dw_w, dw_b, ln_g, ln_b, w_out):
    a, b, z = jnp.split(h @ w_in, 3, axis=-1)
    u = a * jax.nn.sigmoid(b)
    u = lax.conv_general_dilated(
        u, dw_w[:, None, :], window_strides=(1,),
        padding=[(CONV_K // 2, CONV_K // 2)],
        dimension_numbers=('NWC', 'WIO', 'NWC'),
        feature_group_count=CONV_WIDTH) + dw_b
    u = layer_norm(u, ln_g, ln_b)
    u = jax.nn.silu(u) * jax.nn.silu(z)
    return u @ w_out


def na_project(h, w_in):
    B, L, _ = h.shape
    q, k, v, z = jnp.split(h @ w_in, 4, axis=-1)
    heads = lambda t: t.reshape(B, L, N_HEADS, HEAD_DIM)
    return heads(q), heads(k), heads(v), z


def context_attention(q, k, v):
    B, L, H, Dh = q.shape
    nb = L // Q_BLOCK
    q = q * (HEAD_DIM ** -0.5)
    qb = q.reshape(B, nb, Q_BLOCK, H, Dh).swapaxes(0, 1)

    def attend(qi):
        s = jnp.einsum('bqhd,bkhd->bhqk', qi, k).astype(jnp.float32)
        p = jax.nn.softmax(s, axis=-1).astype(v.dtype)
        return jnp.einsum('bhqk,bkhd->bqhd', p, v)

    o = lax.map(attend, qb)
    return o.swapaxes(0, 1).reshape(B, L, H * Dh)


def neighbourhood_attention(q, k, v, k_ctx, v_ctx, rpb):
    B, N, H, Dh = q.shape
    rows = N // GRID_W
    kr = min(NA_ROW_WIN, rows)
    kc = NA_COL_WIN
    ncb = GRID_W // kc
    kc2 = 2 * kc
    blk_start = np.clip(np.arange(ncb) * kc - kc // 2, 0, GRID_W - kc2)
    key_col = blk_start[:, None] + np.arange(kc2)[None, :]
    q_col = np.arange(GRID_W).reshape(ncb, kc)
    q_start = np.clip(q_col - kc // 2, 0, GRID_W - kc)
    in_win = (key_col[:, None, :] >= q_start[..., None]) & (key_col[:, None, :] < q_start[..., None] + kc)
    dc_idx = np.clip(key_col[:, None, :] - q_col[..., None] + kc - 1, 0, 2 * kc - 2)
    mask = jnp.asarray(in_win)[:, :, None, :]

    q = q * (HEAD_DIM ** -0.5)
    k_grid = k.reshape(B, rows, GRID_W, H, Dh)
    v_grid = v.reshape(B, rows, GRID_W, H, Dh)
    q_rows = q.reshape(B, rows, ncb, kc, H, Dh).swapaxes(0, 1)
    n_loc = kr * kc2

    def row_block(args):
        r, qb = args
        rs = jnp.clip(r - kr // 2, 0, rows - kr)
        k_rows = lax.dynamic_slice_in_dim(k_grid, rs, kr, axis=1)
        v_rows = lax.dynamic_slice_in_dim(v_grid, rs, kr, axis=1)
        k_blk = k_rows[:, :, key_col]
        v_blk = v_rows[:, :, key_col]
        s_loc = jnp.einsum('bjqhd,brjkhd->bhjqrk', qb, k_blk).astype(jnp.float32)
        dr_idx = rs + jnp.arange(kr) - r + NA_ROW_WIN - 1
        bias = rpb[:, dr_idx][:, :, dc_idx]
        bias = bias.transpose(0, 2, 3, 1, 4).astype(jnp.float32)
        s_loc = jnp.where(mask, s_loc + bias, NEG_INF).reshape(B, H, ncb, kc, n_loc)
        s_ctx = jnp.einsum('bjqhd,blhd->bhjql', qb, k_ctx).astype(jnp.float32)
        p = jax.nn.softmax(jnp.concatenate([s_loc, s_ctx], axis=-1), axis=-1).astype(v.dtype)
        p_loc = p[..., :n_loc].reshape(B, H, ncb, kc, kr, kc2)
        p_ctx = p[..., n_loc:]
        o = (jnp.einsum('bhjqrk,brjkhd->bjqhd', p_loc, v_blk)
             + jnp.einsum('bhjql,blhd->bjqhd', p_ctx, v_ctx))
        return o.reshape(B, GRID_W, H * Dh)

    out = lax.map(row_block, (jnp.arange(rows), q_rows))
    return out.swapaxes(0, 1).reshape(B, N, H * Dh)


def setup_inputs(seed: int = 0) -> dict:
    key = jax.random.key(seed)
    ks = jax.random.split(key, 20)
    f32 = jnp.float32
    nrm = lambda k, shape, s: jax.random.normal(k, shape, f32) * s
    return {
        'x_prompt': nrm(ks[0], (BATCH, SEQ, D_MODEL), 1.0),
        'x_sample': nrm(ks[1], (DEC_BATCH, DEC_SEQ, D_MODEL), 1.0),
        'c': nrm(ks[2], (DEC_BATCH, D_MODEL), 1.0),
        'cache_k': nrm(ks[3], (DEC_BATCH, N_NA_LAYERS, PAST_LEN, N_HEADS, HEAD_DIM), 1.0),
        'cache_v': nrm(ks[4], (DEC_BATCH, N_NA_LAYERS, PAST_LEN, N_HEADS, HEAD_DIM), 1.0),
        'c_ctx': nrm(ks[5], (D_MODEL,), 1.0),
        'norm_g': 1.0 + nrm(ks[6], (DEPTH, D_MODEL), 0.02),
        'ada_w': nrm(ks[7], (DEPTH, D_MODEL, 3 * D_MODEL), 0.5 * D_MODEL ** -0.5),
        'ada_b': nrm(ks[8], (DEPTH, 3 * D_MODEL), 0.02),
        'conv_w_in': nrm(ks[9], (N_CONV_LAYERS, D_MODEL, 3 * CONV_WIDTH), D_MODEL ** -0.5),
        'conv_dw_w': nrm(ks[10], (N_CONV_LAYERS, CONV_K, CONV_WIDTH), CONV_K ** -0.5),
        'conv_dw_b': nrm(ks[11], (N_CONV_LAYERS, CONV_WIDTH), 0.02),
        'conv_ln_g': 1.0 + nrm(ks[12], (N_CONV_LAYERS, CONV_WIDTH), 0.02),
        'conv_ln_b': nrm(ks[13], (N_CONV_LAYERS, CONV_WIDTH), 0.02),
        'conv_w_out': nrm(ks[14], (N_CONV_LAYERS, CONV_WIDTH, D_MODEL), CONV_WIDTH ** -0.5),
        'na_w_in': nrm(ks[15], (N_NA_LAYERS, D_MODEL, 4 * NA_WIDTH), D_MODEL ** -0.5),
        'na_rpb': nrm(ks[16], (N_NA_LAYERS, N_HEADS, 2 * NA_ROW_WIN - 1, 2 * NA_COL_WIN - 1), 0.1),
        'na_w_out': nrm(ks[17], (N_NA_LAYERS, NA_WIDTH, D_MODEL), NA_WIDTH ** -0.5),
        'final_g': 1.0 + nrm(ks[18], (D_MODEL,), 0.02),
    }


def reference(x_prompt, x_sample, c, cache_k, cache_v, c_ctx, norm_g, ada_w, ada_b,
              conv_w_in, conv_dw_w, conv_dw_b, conv_ln_g, conv_ln_b, conv_w_out,
              na_w_in, na_rpb, na_w_out, final_g):
    x = x_prompt
    new_k, new_v = [], []
    for i in range(DEPTH):
        shift, scale, gate = ada_params(c_ctx[None], ada_w[i], ada_b[i])
        h = modulate(x, norm_g[i], shift, scale)
        j = i // N_MIXERS
        if i % N_MIXERS == 0:
            out = conv_branch(h, conv_w_in[j], conv_dw_w[j], conv_dw_b[j],
                              conv_ln_g[j], conv_ln_b[j], conv_w_out[j])
        else:
            q, k, v, z = na_project(h, na_w_in[j])
            new_k.append(k)
            new_v.append(v)
            out = (context_attention(q, k, v) * jax.nn.silu(z)) @ na_w_out[j]
        x = x + gate[:, None] * out
    y_prompt = rms_norm(x, final_g)
    new_cache_k = jnp.stack(new_k, axis=1)
    new_cache_v = jnp.stack(new_v, axis=1)

    x = x_sample
    for i in range(DEPTH):
        shift, scale, gate = ada_params(c, ada_w[i], ada_b[i])
        h = modulate(x, norm_g[i], shift, scale)
        j = i // N_MIXERS
        if i % N_MIXERS == 0:
            out = conv_branch(h, conv_w_in[j], conv_dw_w[j], conv_dw_b[j],
                              conv_ln_g[j], conv_ln_b[j], conv_w_out[j])
        else:
            q, k, v, z = na_project(h, na_w_in[j])
            att = neighbourhood_attention(q, k, v, cache_k[:, j], cache_v[:, j], na_rpb[j])
            out = (att * jax.nn.silu(z)) @ na_w_out[j]
        x = x + gate[:, None] * out
    y_sample = rms_norm(x, final_g)
    return (y_prompt, y_sample, new_cache_k, new_cache_v)
```

```python
import numpy as np
from contextlib import ExitStack
import concourse.bass as bass
import concourse.mybir as mybir
from concourse.bass_utils import run_bass_kernel_spmd

F32 = mybir.dt.float32
BF16 = mybir.dt.bfloat16
ALU = mybir.AluOpType
AF = mybir.ActivationFunctionType
AX = mybir.AxisListType

NCORES = 8
DEPTH = 4
D = 1024
NCH = 8
LROWS = 26
GW = 64
TS = LROWS * GW
TP = 512
SEQ = 256
CK = 31
PAST = 512
NEG = -30000.0
EPS = 1e-6
DBG = {}


class Sched:
    def __init__(self, nc, stack):
        self.nc = nc
        self.stack = stack
        self.eng = {"pe": nc.tensor, "act": nc.scalar, "dve": nc.vector, "pool": nc.gpsimd, "sp": nc.sync}
        self.sem = {k: stack.enter_context(nc.semaphore("s_" + k)) for k in self.eng}
        self.cnt = {k: 0 for k in self.eng}
        self.waited = {k: {} for k in self.eng}
        self.last_w = {}
        self.readers = {}
        self.dma_cnt = {}
        self.dma_keys = []

    def _wait(self, e, semkey, val):
        kid = semkey if isinstance(semkey, str) else semkey[2]
        if e == "pe" and semkey == "pe":
            return
        if self.waited[e].get(kid, 0) >= val:
            return
        self.waited[e][kid] = val
        sem = self.sem[semkey] if isinstance(semkey, str) else semkey[1]
        self.eng[e].wait_ge(sem, val)

    def _deps(self, e, R, W, relaxed=False):
        for r in R:
            lw = self.last_w.get(r)
            if lw is not None:
                self._wait(e, lw[0], lw[1])
            if isinstance(r, tuple) and r[0] == "PS":
                for rd in self.readers.get(r, ()):
                    if rd[0] != e:
                        self._wait(e, rd[0], rd[1])
        for w in W:
            lw = self.last_w.get(w)
            if lw is not None and not (relaxed and lw[0] == e):
                self._wait(e, lw[0], lw[1])
            for rd in self.readers.get(w, ()):
                if not (relaxed and rd[0] == e):
                    self._wait(e, rd[0], rd[1])

    def _commit(self, tok, R, W):
        for r in R:
            self.readers.setdefault(r, []).append(tok)
        for w in W:
            self.last_w[w] = tok
            self.readers[w] = []

    def op(self, e, fn, R=(), W=()):
        self._deps(e, R, W)
        ins = fn(self.eng[e])
        self.cnt[e] += 1
        ins.then_inc(self.sem[e], 1)
        self._commit((e, self.cnt[e]), R, W)
        return ins

    def new_dma_sem(self, name):
        s = self.stack.enter_context(self.nc.semaphore(name))
        key = ("dma", s, name)
        self.dma_cnt[name] = 0
        self.dma_keys.append(key)
        return key

    def dma(self, q, semkey, out, in_, R=(), W=()):
        self._deps(q, R, W)
        ins = self.eng[q].dma_start(out=out, in_=in_)
        self.dma_cnt[semkey[2]] += 16
        ins.then_inc(semkey[1], 16)
        self._commit((semkey, self.dma_cnt[semkey[2]]), R, W)
        return ins

    def barrier(self):
        for e in self.eng:
            for k in self.eng:
                if k != e and self.cnt[k]:
                    self._wait(e, k, self.cnt[k])
            for key in self.dma_keys:
                if self.dma_cnt[key[2]]:
                    self._wait(e, key, self.dma_cnt[key[2]])
        self.last_w = {}
        self.readers = {}


class Grp:
    def __init__(self, name, gi, T, v, seqs, tile):
        self.name = name
        self.gi = gi
        self.T = T
        self.v = v
        self.seqs = seqs
        self.tiles = []
        for (s0, sl) in seqs:
            t = s0
            while t < s0 + sl:
                n = min(tile, s0 + sl - t)
                self.tiles.append((t, n))
                t += n

    def tidx(self, t):
        for i, (t0, n) in enumerate(self.tiles):
            if t0 <= t < t0 + n:
                return i
        raise ValueError(t)

    def tis(self, t0, n):
        return sorted({self.tidx(t0), self.tidx(t0 + n - 1)})

    def uoff(self, t):
        for si, (s0, sl) in enumerate(self.seqs):
            if s0 <= t < s0 + sl:
                return 15 + t + 15 * si
        raise ValueError(t)

    def ulen(self):
        return self.T + 15 * (len(self.seqs) + 1)


def build_program(depth=DEPTH, sched=None):
    nc = bass.Bass("TRN2", target_bir_lowering=False)
    WREC = []

    def din(name, shape):
        return nc.dram_tensor(name, list(shape), F32, kind="ExternalInput").ap()

    def dout(name, shape):
        return nc.dram_tensor(name, list(shape), F32, kind="ExternalOutput").ap()

    xsT_d = din("xsT", [D, TS])
    xpT_d = din("xpT", [D, TP])
    cvec_d = din("cvec", [128, NCH, 2])
    pv_d = din("pv", [128, 11, NCH])
    adab_d = din("adab", [128, 4, 24])
    dww_d = din("dww", [128, 2, NCH, CK])
    dwwr_d = din("dwwr", [128, 2, NCH, CK])
    ckT_d = din("ckT", [2, D, PAST])
    cv_d = din("cv", [2, PAST, D])
    tm_d = din("tm", [2, 8, 128, 960])
    ident_d = din("ident", [128, 128])
    ada_w_d = din("ada_w", [4, D, 3 * D])
    cab_d = din("cab", [16, D, 256])
    cz_d = din("cz", [2, D, D])
    cwout_d = din("conv_w_out", [2, D, D])
    nwp_d = din("nwp", [16, D, 512])
    nwout_d = din("na_w_out", [2, D, D])

    ysT_d = dout("ysT", [D, TS])
    ypT_d = dout("ypT", [D, TP])
    nkT_d = dout("nkT", [2, D, TP])
    nv_d = dout("nv", [2, TP, D])

    GS = Grp("S", 0, TS, 0, [(0, TS)], 512)
    GP = Grp("P", 1, TP, 1, [(0, SEQ), (SEQ, SEQ)], 256)

    with ExitStack() as st:
        S = Sched(nc, st)

        def sb(name, shape, dt, stack=st):
            return stack.enter_context(nc.sbuf_tensor(name, list(shape), dt))

        XS = sb("XS", [128, NCH, TS], F32)
        XP = sb("XP", [128, NCH, TP], F32)
        X = {0: XS, 1: XP}
        HT = sb("HT", [128, NCH, TS], BF16)
        GT = sb("GT", [128, NCH, TS], BF16)
        NWB = 3
        WB = [sb("WB%d" % k, [128, NCH, 512], BF16) for k in range(NWB)]
        SQR = [sb("SQ%d" % k, [128, 512], BF16) for k in range(2)]
        HTP = sb("HTP", [128, NCH, TP], BF16)
        HTG = {0: HT, 1: HTP}
        NT = 5
        TT = [sb("T%d" % k, [128, 512], F32) for k in range(NT)]
        TR = [sb("TR%d" % k, [128, 512], F32) for k in range(2)]
        IDB = sb("IDB", [128, 128], BF16)
        ONES = sb("ONES", [128, 128], BF16)
        EPSB = sb("EPSB", [128, 1], F32)
        NEG1 = sb("NEG1", [128, 1], F32)
        CV = sb("CV", [128, NCH, 2], F32)
        SCT = sb("SCT", [128, NCH, 2], BF16)
        PV = sb("PVEC", [128, 11, NCH], F32)
        ADAB = sb("ADAB", [128, 4, 24], F32)
        DWW = sb("DWW", [128, 2, NCH, CK], BF16)
        DWWR = sb("DWWR", [128, 2, NCH, CK], BF16)
        ID64 = sb("ID64", [128, 64], BF16)
        MODS = sb("MODS", [128, 4, 2, 24], F32)
        AMOD = sb("AMOD", [128, 4, 2, NCH], F32)
        GATE = sb("GATE", [128, 4, 2, NCH], F32)
        MSTG = sb("MSTG", [128, 48], F32)

        PS = [st.enter_context(nc.psum_tensor("PS%d" % k, [128, 1024], F32)) for k in range(4)]

        dq = {k: S.new_dma_sem("dq_" + k) for k in ("ld", "ld2", "w0", "w1", "w2", "aux", "idb", "v1", "u1", "u1b", "kc0", "kc1", "vc0", "vc1", "tm0", "tm1", "ko0", "ko1")}
        wsem = [dq["w0"], dq["w1"], dq["w2"]]
        tsem = [S.new_dma_sem("dq_t%d" % k) for k in range(NT)]

        state = {"t": 0, "wb": 0, "pp": 0, "sp": 0, "tr": 0}

        def tmp():
            k = state["t"] % NT
            state["t"] += 1
            return TT[k], ("T", k)

        def tmpr():
            k = state["tr"] % 2
            state["tr"] += 1
            return TR[k], ("TR", k)

        DR = {"ada_w": ada_w_d, "cab": cab_d, "cz": cz_d, "cwout": cwout_d, "nwp": nwp_d, "nwout": nwout_d}
        wst = {"idx": 0, "issued": 0}

        def w_issue(idx, spec):
            k = idx % NWB
            for (dlo, width, wname, li, slo) in spec:
                S.dma("pool", wsem[k], WB[k][:, :, dlo:dlo + width], wview(DR[wname][li])[:, :, slo:slo + width], W=[("WB", k)])

        def wnext(spec):
            idx = wst["idx"]
            wst["idx"] += 1
            if sched is None:
                WREC.append(spec)
                w_issue(idx, spec)
            else:
                assert sched[idx] == spec, (idx, sched[idx], spec)
                while wst["issued"] <= min(idx + 2, len(sched) - 1):
                    w_issue(wst["issued"], sched[wst["issued"]])
                    wst["issued"] += 1
            return WB[idx % NWB], ("WB", idx % NWB)

        PP_ALL = [(0, 0), (0, 1), (1, 0), (1, 1)]
        state["pool"] = PP_ALL

        def projps():
            pool_ = state["pool"]
            t, h = pool_[state["pp"] % len(pool_)]
            state["pp"] += 1
            return PS[t][:, h * 512:h * 512 + 512], ("PS", t, h)

        SP_ALL = [(2, 0), (2, 1)]
        state["spool"] = SP_ALL

        def statps():
            pool_ = state["spool"]
            t, h = pool_[state["sp"] % len(pool_)]
            state["sp"] += 1
            return PS[t][:, h * 512:h * 512 + 512], ("PS", t, h)

        def wview(w2d):
            return w2d.rearrange("(c p) n -> p c n", p=128)

        S.dma("sp", dq["aux"], CV[:], cvec_d, W=["CV"])
        S.dma("sp", dq["aux"], PV[:], pv_d, W=["PVEC"])
        S.dma("sp", dq["aux"], ADAB[:], adab_d, W=["ADAB"])
        S.dma("pool", dq["idb"], DWW[:], dww_d, W=["DWW"])
        S.dma("pool", dq["idb"], DWWR[:], dwwr_d, W=["DWWR"])
        S.dma("pool", dq["idb"], IDB[:], ident_d, W=["IDB"])
        S.op("dve", lambda e: e.memset(ONES[:], 1.0 / D), W=["ONES"])
        S.op("dve", lambda e: e.memset(EPSB[:], EPS), W=["EPSB"])
        S.op("dve", lambda e: e.memset(NEG1[:], -1.0), W=["NEG1"])
        S.barrier()
        S.dma("sp", dq["ld"], XS[:], xsT_d.rearrange("(c p) t -> p c t", p=128),
              W=[("X", 0, c) for c in range(NCH)])
        S.dma("sp", dq["ld2"], XP[:], xpT_d.rearrange("(c p) t -> p c t", p=128),
              W=[("X", 1, c) for c in range(NCH)])
        S.op("act", lambda e: e.activation(out=SCT[:], in_=CV[:], func=AF.Silu), R=["CV"], W=["SCT"])
        S.op("dve", lambda e: e.tensor_tensor(out=ID64[:], in0=IDB[:, 0:64], in1=IDB[:, 64:128], op=ALU.add), R=["IDB"], W=["ID64"])

        ada_q = [(i, blk) for i in range(depth) for blk in range(6)]

        def ada_step():
            if not ada_q:
                return
            i, blk = ada_q.pop(0)
            wb, wk = wnext([(0, 512, "ada_w", i, blk * 512)])
            aps = PS[3][:, 0:8]
            for jn in range(4):
                for c in range(NCH):
                    S.op("pe", lambda e, c=c, jn=jn, wb=wb: e.matmul(
                        aps[:, 2 * jn:2 * jn + 2], lhsT=wb[:, c, jn * 128:(jn + 1) * 128], rhs=SCT[:, c, :],
                        start=(c == 0), stop=(c == NCH - 1)),
                        R=[wk, "SCT"], W=[("PS", 3, 0)])
            S.op("dve", lambda e: e.tensor_copy(out=MSTG[:, 8 * blk:8 * blk + 8], in_=aps), R=[("PS", 3, 0)], W=["MSTG"])
            if blk == 5:
                m3 = MSTG[:].rearrange("p (n v) -> p n v", v=2)
                for v in range(2):
                    S.op("dve", lambda e, v=v: e.tensor_tensor(out=MODS[:, i, v, :], in0=m3[:, :, v], in1=ADAB[:, i, :], op=ALU.add),
                         R=["MSTG", "ADAB"], W=[("MODS", i)])
                    S.op("dve", lambda e, v=v: e.scalar_tensor_tensor(
                        out=AMOD[:, i, v, :], in0=MODS[:, i, v, 8:16], scalar=1.0, in1=PV[:, i, :],
                        op0=ALU.add, op1=ALU.mult), R=[("MODS", i), "PVEC"], W=[("AMOD", i)])
                    S.op("dve", lambda e, v=v: e.tensor_scalar(out=GATE[:, i, v, :], in0=MODS[:, i, v, 16:24],
                                                               scalar1=(0.5 if i % 2 == 1 else 1.0), scalar2=None, op0=ALU.mult),
                         R=[("MODS", i)], W=[("GATE", i)])

        def ada_flush(upto):
            while ada_q and ada_q[0][0] <= upto:
                ada_step()

        ada_flush(0)

        def rms_rstd(g, t0, nt, xr):
            Xg = X[g.gi]
            ps, pk = statps()
            for c in range(NCH):
                sq = SQR[c % 2]
                S.op("act", lambda e, c=c, sq=sq: e.activation(out=sq[:, :nt], in_=Xg[:, c, t0:t0 + nt], func=AF.Square),
                     R=[xr(c)], W=[("SQ", c % 2)])
                S.op("pe", lambda e, c=c, sq=sq: e.matmul(ps[:, :nt], lhsT=ONES[:], rhs=sq[:, :nt],
                                                          start=(c == 0), stop=(c == NCH - 1)),
                     R=["ONES", ("SQ", c % 2)], W=[pk])
            t1, k1 = tmpr()
            S.op("act", lambda e: e.activation(out=t1[:, :nt], in_=ps[:, :nt], func=AF.Sqrt, bias=EPSB[:, 0:1], scale=1.0),
                 R=[pk, "EPSB"], W=[k1])
            S.op("dve", lambda e: e.reciprocal(out=t1[:, :nt], in_=t1[:, :nt]), R=[k1], W=[k1])
            return t1, k1

        def modulate_gen(g, i):
            Xg = X[g.gi]
            for ti, (t0, nt) in enumerate(g.tiles):
                t1, k1 = rms_rstd(g, t0, nt, lambda c: ("X", g.gi, c))
                yield
                for c in range(NCH):
                    t2, k2 = tmp()
                    S.op("dve", lambda e, c=c, t2=t2: e.scalar_tensor_tensor(
                        out=t2[:, :nt], in0=Xg[:, c, t0:t0 + nt], scalar=AMOD[:, i, g.v, c:c + 1], in1=t1[:, :nt],
                        op0=ALU.mult, op1=ALU.mult), R=[("X", g.gi, c), ("AMOD", i), k1], W=[k2])
                    S.op("act", lambda e, c=c, t2=t2: e.activation(
                        out=HTG[g.gi][:, c, t0:t0 + nt], in_=t2[:, :nt], func=AF.Identity,
                        bias=MODS[:, i, g.v, c:c + 1], scale=1.0), R=[k2, ("MODS", i)], W=[("HT", g.gi, c, ti)])
                    yield

        def modulate(g, i):
            for _ in modulate_gen(g, i):
                pass

        def proj_mm(wb, wk, col0, g, ti):
            t0, nt = g.tiles[ti]
            ps, pk = projps()
            for c in range(NCH):
                S.op("pe", lambda e, c=c: e.matmul(ps[:, :nt], lhsT=wb[:, c, col0:col0 + 128], rhs=HTG[g.gi][:, c, t0:t0 + nt],
                                                    start=(c == 0), stop=(c == NCH - 1)),
                     R=[wk, ("HT", g.gi, c, ti)], W=[pk])
            return ps, pk, t0, nt, ti

        def proj_fm(wb, wk, col0, g, ti, evac):
            evac(*proj_mm(wb, wk, col0, g, ti))

        def out_proj(g, i, wname, j, bg=None, bg_steps=0):
            Xg = X[g.gi]
            ngroups = NCH * len(g.tiles)
            per = -(-bg_steps // ngroups) if bg is not None else 0
            for blk in range(2):
                wb, wk = wnext([(0, 512, wname, j, blk * 512)])
                for jm in range(4):
                    m = blk * 4 + jm
                    for ti, (t0, nt) in enumerate(g.tiles):
                        ps, pk = projps()
                        for n in range(NCH):
                            S.op("pe", lambda e, n=n: e.matmul(ps[:, :nt], lhsT=wb[:, n, jm * 128:(jm + 1) * 128],
                                                                rhs=GT[:, n, t0:t0 + nt], start=(n == 0), stop=(n == NCH - 1)),
                                 R=[wk, ("GT", n, ti)], W=[pk])
                        S.op("dve", lambda e: e.scalar_tensor_tensor(
                            out=Xg[:, m, t0:t0 + nt], in0=ps[:, :nt], scalar=GATE[:, i, g.v, m:m + 1],
                            in1=Xg[:, m, t0:t0 + nt], op0=ALU.mult, op1=ALU.add),
                            R=[pk, ("GATE", i), ("X", g.gi, m)], W=[("X", g.gi, m)])
                        for _ in range(per):
                            if bg is not None and next(bg, "end") == "end":
                                bg = None
            if bg is not None:
                for _ in bg:
                    pass

        def conv_layer(i, g, lst, hook):
            j = i // 2
            UL = g.ulen()
            U = [sb("U%d_%d_%d" % (i, g.gi, k), [128, UL], BF16, lst) for k in range(2)]
            NPB = 2 if g.gi == 1 else 1
            U1s = [sb("U1_%d_%d_%d" % (i, g.gi, k), [128, UL], BF16, lst) for k in range(NPB)]
            DGAs = [sb("DGA%d_%d_%d" % (i, g.gi, k), [128, CK, 64], BF16, lst) for k in range(NPB)]
            DGBs = [sb("DGB%d_%d_%d" % (i, g.gi, k), [128, CK, 64], BF16, lst) for k in range(NPB)]
            NS0 = (CK + 1) // 2
            MR = sb("MR%d_%d" % (i, g.gi), [128, 2, g.T], F32, lst)
            if DBG.get('mem'): print('conv mem remaining', g.name, nc.sbuf_bytes_remaining)
            for k in range(2):
                S.op("pool", lambda e, k=k: e.memset(U[k][:], 0.0), W=[("U", k)])
            def glu(n):
                wb, wk = wnext([(0, 256, "cab", j * 8 + n, 0)])
                ub = n % 2
                for ti, (t0, nt) in enumerate(g.tiles):
                    pa, ka = projps()
                    pb, kb = projps()
                    for (ps, pk, col0) in ((pa, ka, 0), (pb, kb, 128)):
                        for c in range(NCH):
                            S.op("pe", lambda e, c=c, ps=ps, col0=col0: e.matmul(
                                ps[:, :nt], lhsT=wb[:, c, col0:col0 + 128], rhs=HTG[g.gi][:, c, t0:t0 + nt],
                                start=(c == 0), stop=(c == NCH - 1)), R=[wk, ("HT", g.gi, c, ti)], W=[pk])
                    t1, k1 = tmp()
                    S.op("act", lambda e: e.activation(out=t1[:, :nt], in_=pb[:, :nt], func=AF.Sigmoid), R=[kb], W=[k1])
                    uo = g.uoff(t0)
                    S.op("dve", lambda e: e.tensor_tensor(out=U[ub][:, uo:uo + nt], in0=pa[:, :nt], in1=t1[:, :nt], op=ALU.mult),
                         R=[ka, k1], W=[("U", ub)])

            def prep(n):
                ub = n % 2
                pb_ = n % NPB
                U1, DGA, DGB = U1s[pb_], DGAs[pb_], DGBs[pb_]
                S.op("dve", lambda e: e.tensor_tensor(
                    out=DGA[:], in0=ID64[:].unsqueeze(1).to_broadcast([128, CK, 64]),
                    in1=DWW[:, j, n, :].unsqueeze(2).to_broadcast([128, CK, 64]), op=ALU.mult),
                    R=["ID64", "DWW"], W=[("DGA", pb_)])
                S.op("dve", lambda e: e.tensor_tensor(
                    out=DGB[:], in0=ID64[:].unsqueeze(1).to_broadcast([128, CK, 64]),
                    in1=DWWR[:, j, n, :].unsqueeze(2).to_broadcast([128, CK, 64]), op=ALU.mult),
                    R=["ID64", "DWWR"], W=[("DGB", pb_)])
                S.dma("sp", dq["u1b" if pb_ else "u1"], U1[0:64, :], U[ub][64:128, :], R=[("U", ub)], W=[("U1", pb_)])
                S.dma("sp", dq["u1b" if pb_ else "u1"], U1[64:128, :], U[ub][0:64, :], R=[("U", ub)], W=[("U1", pb_)])

            def conv(n):
                ub = n % 2
                for ti, (t0, nt) in enumerate(g.tiles):
                    pt_ = 2 + (ti % 2)
                    banks = [(PS[pt_][:, 0:512], ("PS", pt_, 0)), (PS[pt_][:, 512:1024], ("PS", pt_, 1))]
                    uo = g.uoff(t0)
                    for sl in range(NS0):
                        for (ih, jh) in ((0, 0), (1, 1), (1, 0), (0, 1)):
                            own = (ih == jh)
                            k = sl if own else NS0 + sl
                            if k >= CK:
                                continue
                            first = (sl == 0)
                            last = (k == NS0 - 1) if own else (k == CK - 1)
                            src, skey = (U[ub], ("U", ub)) if own else (U1s[n % NPB], ("U1", n % NPB))
                            dg, dkey = (DGAs[n % NPB], ("DGA", n % NPB)) if own else (DGBs[n % NPB], ("DGB", n % NPB))
                            bk, bkey = banks[ih]
                            S.op("pe", lambda e, ih=ih, jh=jh, k=k, src=src, dg=dg, bk=bk, first=first, last=last: e.matmul(
                                bk[64 * jh:64 * jh + 64, :nt], lhsT=dg[64 * ih:64 * ih + 64, k, :],
                                rhs=src[64 * ih:64 * ih + 64, uo + k - 15:uo + k - 15 + nt],
                                start=first, stop=last, tile_position=((64 * ih, 64 * jh) if (ih or jh) else None)),
                                R=[dkey, skey], W=[bkey])
                    t1, k1 = tmp()
                    t2, k2 = tmp()
                    S.op("act", lambda e: e.activation(out=t1[:, :nt], in_=banks[0][0][:, :nt], func=AF.Identity,
                                                       bias=PV[:, 5 + j, n:n + 1], scale=1.0),
                         R=[banks[0][1], "PVEC"], W=[k1])
                    S.op("dve", lambda e: e.tensor_copy(out=t2[:, :nt], in_=banks[1][0][:, :nt]), R=[banks[1][1]], W=[k2])
                    S.op("pool", lambda e: e.tensor_tensor(out=GT[:, n, t0:t0 + nt], in0=t1[:, :nt], in1=t2[:, :nt], op=ALU.add),
                         R=[k1, k2], W=[("GT", n, ti)])

            bg0, bg_total = hook()
            bgs = [bg0]
            bg_per = -(-bg_total * 2 // (3 * NCH)) if bg0 is not None else 0
            glu(0)
            prep(0)
            for n in range(NCH):
                if n + 1 < NCH:
                    glu(n + 1)
                    if NPB > 1:
                        prep(n + 1)
                conv(n)
                if n + 1 < NCH and NPB == 1:
                    prep(n + 1)
                ada_step()
                for _ in range(bg_per):
                    if bgs[0] is not None and next(bgs[0], "end") == "end":
                        bgs[0] = None

            state["spool"] = SP_ALL
            for ti, (t0, nt) in enumerate(g.tiles):
                pm, km = statps()
                pq, kq = statps()
                for c in range(NCH):
                    S.op("pe", lambda e, c=c: e.matmul(pm[:, :nt], lhsT=ONES[:], rhs=GT[:, c, t0:t0 + nt],
                                                        start=(c == 0), stop=(c == NCH - 1)), R=["ONES", ("GT", c, ti)], W=[km])
                for c in range(NCH):
                    sq = SQR[c % 2]
                    S.op("act", lambda e, c=c, sq=sq: e.activation(out=sq[:, :nt], in_=GT[:, c, t0:t0 + nt], func=AF.Square),
                         R=[("GT", c, ti)], W=[("SQ", c % 2)])
                    S.op("pe", lambda e, c=c, sq=sq: e.matmul(pq[:, :nt], lhsT=ONES[:], rhs=sq[:, :nt],
                                                              start=(c == 0), stop=(c == NCH - 1)), R=["ONES", ("SQ", c % 2)], W=[kq])
                S.op("act", lambda e: e.activation(out=MR[:, 0, t0:t0 + nt], in_=pm[:, :nt], func=AF.Copy), R=[km], W=[("MR", 0, ti)])
                t1, k1 = tmp()
                S.op("dve", lambda e: e.tensor_tensor(out=t1[:, :nt], in0=MR[:, 0, t0:t0 + nt], in1=MR[:, 0, t0:t0 + nt], op=ALU.mult),
                     R=[("MR", 0, ti)], W=[k1])
                S.op("dve", lambda e: e.tensor_tensor(out=t1[:, :nt], in0=pq[:, :nt], in1=t1[:, :nt], op=ALU.subtract),
                     R=[kq, k1], W=[k1])
                S.op("act", lambda e: e.activation(out=MR[:, 1, t0:t0 + nt], in_=t1[:, :nt], func=AF.Sqrt, bias=EPSB[:, 0:1], scale=1.0),
                     R=[k1, "EPSB"], W=[("MR", 1, ti)])
                S.op("dve", lambda e: e.reciprocal(out=MR[:, 1, t0:t0 + nt], in_=MR[:, 1, t0:t0 + nt]),
                     R=[("MR", 1, ti)], W=[("MR", 1, ti)])

            for blk in range(2):
                wb, wk = wnext([(0, 512, "cz", j, blk * 512)])
                for jn in range(4):
                    n = blk * 4 + jn
                    for ti in range(len(g.tiles)):
                        def evac(ps, pk, t0, nt, ti, n=n):
                            t1, k1 = tmp()
                            S.op("act", lambda e: e.activation(out=t1[:, :nt], in_=ps[:, :nt], func=AF.Silu), R=[pk], W=[k1])
                            t2, k2 = tmp()
                            S.op("dve", lambda e: e.tensor_tensor(out=t2[:, :nt], in0=GT[:, n, t0:t0 + nt], in1=MR[:, 0, t0:t0 + nt], op=ALU.subtract),
                                 R=[("GT", n, ti), ("MR", 0, ti)], W=[k2])
                            S.op("dve", lambda e: e.tensor_tensor(out=t2[:, :nt], in0=t2[:, :nt], in1=MR[:, 1, t0:t0 + nt], op=ALU.mult),
                                 R=[k2, ("MR", 1, ti)], W=[k2])
                            t3, k3 = tmp()
                            S.op("act", lambda e: e.activation(out=t3[:, :nt], in_=t2[:, :nt], func=AF.Silu,
                                                               bias=PV[:, 9 + j, n:n + 1], scale=PV[:, 7 + j, n:n + 1]),
                                 R=[k2, "PVEC"], W=[k3])
                            S.op("dve", lambda e: e.tensor_tensor(out=GT[:, n, t0:t0 + nt], in0=t3[:, :nt], in1=t1[:, :nt], op=ALU.mult),
                                 R=[k3, k1], W=[("GT", n, ti)])
                        proj_fm(wb, wk, jn * 128, g, ti, evac)

            out_proj(g, i, "cwout", j, bgs[0], bg_total // 3)

        def na_layer(i, g, lst, hook):
            j = i // 2
            is_s = (g.gi == 0)
            T = g.T
            NB = (T + 127) // 128
            NSET = 1 if is_s else 2
            QTs = [sb("QT%d_%d_%d" % (i, g.gi, k), [128, T], BF16, lst) for k in range(NSET)]
            KTs = [sb("KT%d_%d_%d" % (i, g.gi, k), [128, T], BF16, lst) for k in range(NSET)]
            SZs = [sb("SZ%d_%d_%d" % (i, g.gi, k), [128, T], BF16, lst) for k in range(NSET)]
            V0s = [sb("V0%d_%d_%d" % (i, g.gi, k), [128, NB, 128], BF16, lst) for k in range(NSET)]
            V1 = sb("V1%d_%d" % (i, g.gi), [128, NB, 128], BF16, lst) if is_s else None
            if is_s:
                KC2 = [sb("KC%d_%d" % (i, k), [128, PAST], BF16, lst) for k in range(2)]
                VC2 = [sb("VC%d_%d" % (i, k), [128, PAST // 128, 128], BF16, lst) for k in range(2)]
                TM2 = [sb("TM%d_%d" % (i, k), [128, 960], BF16, lst) for k in range(2)]

                def load_ctx(n):
                    k = n % 2
                    S.dma("pool", dq["kc%d" % k], KC2[k][:], ckT_d[j, n * 128:(n + 1) * 128, :], W=[("KC", k)])
                    S.dma("pool", dq["vc%d" % k], VC2[k][:], cv_d[j].rearrange("(b p) f -> p b f", p=128)[:, :, n * 128:(n + 1) * 128], W=[("VC", k)])
                    S.dma("pool", dq["tm%d" % k], TM2[k][:], tm_d[j, n], W=[("TM", k)])
            PB_ = [sb("P%d_%d_%d" % (i, g.gi, k), [128, 1024], BF16, lst) for k in range(3)]
            PTB = [sb("PT%d_%d_%d" % (i, g.gi, k), [128, 1024], BF16, lst) for k in range(2)]
            DD = [sb("DD%d_%d_%d" % (i, g.gi, k), [128, 128], BF16, lst) for k in range(3)]
            STT = [sb("ST%d_%d_%d" % (i, g.gi, k), [128, 4], F32, lst) for k in range(2)]
            KO = [sb("KO%d_%d_%d" % (i, g.gi, k), [128, 512], F32, lst) for k in range(2)] if not is_s else None

            if DBG.get('mem'): print('NA mem remaining', g.name, nc.sbuf_bytes_remaining)
            kos = {"k": 0}

            npairs = DBG.get('pairs', NCH)

            def pair_body(n, filler):
                sx = n % NSET
                QT, KT, SZ, V0 = QTs[sx], KTs[sx], SZs[sx], V0s[sx]
                wb, wk = wnext([(0, 512, "nwp", j * 8 + n, 0)])
                if is_s:
                    if n == 0:
                        load_ctx(0)
                    KC, VC, TM = KC2[n % 2], VC2[n % 2], TM2[n % 2]
                    KCk, VCk, TMk = ("KC", n % 2), ("VC", n % 2), ("TM", n % 2)

                def evq(ps, pk, t0, nt, ti):
                    S.op("act", lambda e: e.activation(out=QT[:, t0:t0 + nt], in_=ps[:, :nt], func=AF.Identity, scale=0.125),
                         R=[pk], W=[("QT", sx, ti)])

                def evk(ps, pk, t0, nt, ti):
                    S.op("dve", lambda e: e.tensor_copy(out=KT[:, t0:t0 + nt], in_=ps[:, :nt]), R=[pk], W=[("KT", sx, ti)])
                    if not is_s and 'ko' not in DBG.get('skip', ''):
                        kb = kos["k"] % 2
                        kos["k"] += 1
                        S.op("act", lambda e: e.activation(out=KO[kb][:, :nt], in_=ps[:, :nt], func=AF.Copy), R=[pk], W=[("KO", kb)])
                        S.dma("sp", dq["ko%d" % kb], nkT_d[j, n * 128:(n + 1) * 128, t0:t0 + nt], KO[kb][:, :nt], R=[("KO", kb)])

                def evz(ps, pk, t0, nt, ti):
                    t1, k1 = tmp()
                    S.op("act", lambda e: e.activation(out=t1[:, :nt], in_=ps[:, :nt], func=AF.Tanh, scale=0.5), R=[pk], W=[k1])
                    S.op("dve", lambda e: e.scalar_tensor_tensor(out=SZ[:, t0:t0 + nt], in0=t1[:, :nt], scalar=1.0, in1=ps[:, :nt],
                                                                 op0=ALU.add, op1=ALU.mult), R=[k1, pk], W=[("SZ", sx, ti)])

                for (col0, ev) in ((0, evq), (128, evk)):
                    for ti in range(len(g.tiles)):
                        r_ = proj_mm(wb, wk, col0, g, ti)
                        yield "p"
                        ev(*r_)
                yield "qk"
                if is_s and n + 1 < NCH:
                    load_ctx(n + 1)
                units = []
                if is_s:
                    for r in range(LROWS):
                        rs = min(max(r - 4, 0), LROWS - 8)
                        vv, vn = (V0, "V0") if rs % 2 == 0 else (V1, "V1")
                        vb0 = rs // 2
                        units.append(dict(q0=r * GW, k0=rs * GW, nkl=512, tmo=(rs - r + 7) * GW, ctx=True,
                                          vv=vv, vn=vn, vblocks=[vb0 + x for x in range(4)]))
                else:
                    for p in range(2):
                        for qb in range(SEQ // 64):
                            units.append(dict(q0=p * SEQ + qb * 64, k0=p * SEQ, nkl=SEQ, tmo=None, ctx=False,
                                              vv=V0, vn="V0", vblocks=[2 * p, 2 * p + 1]))

                def emit_A(u, ui):
                    sp_ = PS[ui % 2]
                    sk = [("PS", ui % 2, 0), ("PS", ui % 2, 1)]
                    q0, k0, nkl = u["q0"], u["k0"], u["nkl"]
                    qti = g.tis(q0, 64)
                    kti = g.tis(k0, nkl)
                    hbs = [(0, 64, None), (64, 128, (64, 64))]
                    for (lo, hi, tp) in hbs:
                        S.op("pe", lambda e, lo=lo, hi=hi, tp=tp: e.matmul(
                            sp_[lo:hi, 0:nkl], lhsT=QT[lo:hi, q0:q0 + 64], rhs=KT[lo:hi, k0:k0 + nkl],
                            start=True, stop=(u["tmo"] is None), tile_position=tp),
                            R=[("QT", sx, x) for x in qti] + [("KT", sx, x) for x in kti], W=[sk[0]])
                    if u["tmo"] is not None:
                        for (lo, hi, tp) in hbs:
                            S.op("pe", lambda e, lo=lo, hi=hi, tp=tp: e.matmul(
                                sp_[lo:hi, 0:nkl], lhsT=IDB[lo:hi, lo:hi], rhs=TM[lo:hi, u["tmo"]:u["tmo"] + nkl],
                                start=False, stop=True, tile_position=tp), R=["IDB", TMk], W=[sk[0]])
                    if u["ctx"]:
                        for (lo, hi, tp) in hbs:
                            S.op("pe", lambda e, lo=lo, hi=hi, tp=tp: e.matmul(
                                sp_[lo:hi, 512:512 + PAST], lhsT=QT[lo:hi, q0:q0 + 64], rhs=KC[lo:hi, :],
                                start=True, stop=True, tile_position=tp),
                                R=[("QT", sx, x) for x in qti] + [KCk], W=[sk[1]])

                def _sm(u, ui):
                    sp_ = PS[ui % 2]
                    sk = [("PS", ui % 2, 0), ("PS", ui % 2, 1)]
                    nk = 1024 if u["ctx"] else u["nkl"]
                    rsk = sk if u["ctx"] else sk[:1]
                    return sp_, nk, rsk, STT[ui % 2], ("ST", ui % 2), PB_[ui % 3], ("P", ui % 3)

                def emit_B(u, ui):
                    sp_, nk, rsk, stt, stk, P_, pk_ = _sm(u, ui)
                    S.op("dve", lambda e: e.reduce_max(out=stt[:, 0:1], in_=sp_[:, 0:nk], axis=AX.X, negate=True), R=rsk, W=[stk])

                def emit_C(u, ui):
                    sp_, nk, rsk, stt, stk, P_, pk_ = _sm(u, ui)
                    S.op("act", lambda e: e.activation(out=P_[:, 0:nk], in_=sp_[:, 0:nk], func=AF.Exp, bias=stt[:, 0:1], scale=1.0,
                                                       accum_out=stt[:, 2:3]), R=rsk + [stk], W=[pk_, ("SM", ui % 2)])

                def emit_D(u, ui):
                    sp_, nk, rsk, stt, stk, P_, pk_ = _sm(u, ui)
                    smk = ("SM", ui % 2)
                    S.op("pool", lambda e: e.tensor_tensor(out=stt[:, 3:4], in0=stt[:, 2:3], in1=NEG1[:], op=ALU.pow),
                         R=[smk, "NEG1"], W=[("RS", ui % 2)])
                    S.op("pool", lambda e: e.tensor_tensor(out=DD[ui % 3][:], in0=IDB[:], in1=stt[:, 3:4].to_broadcast([128, 128]), op=ALU.mult),
                         R=["IDB", ("RS", ui % 2)], W=[("DD", ui % 3)])

                def emit_E(u, ui):
                    nk = 1024 if u["ctx"] else u["nkl"]
                    P_, pk_ = PB_[ui % 3], ("P", ui % 3)
                    for jj in range(nk // 128):
                        S.op("pe", lambda e, jj=jj: e.matmul(PS[2][:, jj * 128:(jj + 1) * 128], lhsT=P_[:, jj * 128:(jj + 1) * 128],
                                                              rhs=DD[ui % 3][:], start=True, stop=True),
                             R=[pk_, ("DD", ui % 3)], W=[("PS", 2, jj // 4)])

                def emit_F(u, ui):
                    nk = 1024 if u["ctx"] else u["nkl"]
                    PT_, ptk = PTB[ui % 2], ("PT", ui % 2)
                    h0 = min(nk, 512)
                    S.op("act", lambda e: e.activation(out=PT_[:, 0:h0], in_=PS[2][:, 0:h0], func=AF.Copy),
                         R=[("PS", 2, 0)], W=[(ptk, 0)])
                    if nk > 512:
                        S.op("dve", lambda e: e.tensor_copy(out=PT_[:, 512:nk], in_=PS[2][:, 512:nk]),
                             R=[("PS", 2, 1)], W=[(ptk, 1)])

                def emit_G(u, ui):
                    slot = ui % 8
                    bank = (ui // 8) % 2
                    ak = ("PS", 3, bank)
                    ap_ = PS[3][:, bank * 512:(bank + 1) * 512]
                    PT_, ptk = PTB[ui % 2], ("PT", ui % 2)
                    chunks = [(u["vv"], u["vn"], b, jj) for jj, b in enumerate(u["vblocks"])]
                    if u["ctx"]:
                        chunks += [(VC, None, b, 4 + b) for b in range(PAST // 128)]
                    for ci, (vv, vn, b, jj) in enumerate(chunks):
                        rk = [(vn, sx, b)] if vn is not None else [VCk]
                        for (lo, hi) in ((0, 64), (64, 128)):
                            S.op("pe", lambda e, vv=vv, b=b, jj=jj, lo=lo, hi=hi, ci=ci: e.matmul(
                                ap_[lo:hi, slot * 64:(slot + 1) * 64], lhsT=vv[:, b, lo:hi], rhs=PT_[:, jj * 128 + lo:jj * 128 + hi],
                                start=(ci == 0), stop=(ci == len(chunks) - 1), tile_position=((0, lo) if lo else None)),
                                R=rk + [(ptk, jj // 4)], W=[ak])
                    if slot == 7 or ui == nu - 1:
                        qs = units[ui - slot]["q0"]
                        nq = (slot + 1) * 64
                        S.op("dve", lambda e, qs=qs, nq=nq, ap_=ap_: e.tensor_tensor(
                            out=GT[:, n, qs:qs + nq], in0=ap_[:, 0:nq], in1=SZ[:, qs:qs + nq], op=ALU.mult),
                            R=[ak] + [("SZ", sx, x) for x in g.tis(qs, nq)], W=[("GT", n, x) for x in g.tis(qs, nq)])

                units = units[:DBG.get('units', len(units))]
                nu = len(units)

                def prologue():
                    emit_A(units[0], 0)
                    if nu > 1:
                        emit_A(units[1], 1)
                    emit_B(units[0], 0)
                    emit_C(units[0], 0)
                    emit_D(units[0], 0)

                if is_s:
                    prologue()
                    state["pool"] = [(2, 0), (2, 1), (3, 0), (3, 1)]
                for ti in range(len(g.tiles)):
                    r_ = proj_mm(wb, wk, 384, g, ti)
                    yield "p"
                    evz(*r_)

                for (VV, off, vname) in ((V0, 0, "V0"),):
                    blocks = []
                    b = 0
                    while off + 128 * b < T:
                        blocks.append((b, off + 128 * b, min(128, T - off - 128 * b)))
                        b += 1
                    for q0 in range(0, len(blocks), 4):
                        quad = blocks[q0:q0 + 4]
                        ps, pk = projps()
                        for qi, (b, t0, m) in enumerate(quad):
                            for c in range(NCH):
                                S.op("pe", lambda e, c=c, qi=qi, t0=t0, m=m: e.matmul(
                                    ps[:m, qi * 128:(qi + 1) * 128], lhsT=HTG[g.gi][:, c, t0:t0 + m], rhs=wb[:, c, 256:384],
                                    start=(c == 0), stop=(c == NCH - 1)),
                                    R=[wk] + [("HT", g.gi, c, x) for x in g.tis(t0, m)], W=[pk])
                        yield "p"
                        for qi, (b, t0, m) in enumerate(quad):
                            S.op("dve", lambda e, qi=qi, b=b, m=m: e.tensor_copy(out=VV[:m, b, :], in_=ps[:m, qi * 128:(qi + 1) * 128]),
                                 R=[pk], W=[(vname, sx, b)])
                            if not is_s and 'vo' not in DBG.get('skip', ''):
                                kb = kos["k"] % 2
                                kos["k"] += 1
                                S.op("act", lambda e, qi=qi, m=m, kb=kb: e.activation(out=KO[kb][:m, 0:128], in_=ps[:m, qi * 128:(qi + 1) * 128], func=AF.Copy),
                                     R=[pk], W=[("KO", kb)])
                                S.dma("sp", dq["ko%d" % kb], nv_d[j, t0:t0 + m, n * 128:(n + 1) * 128], KO[kb][:m, 0:128], R=[("KO", kb)])

                if is_s:
                    S.dma("sp", dq["v1"], V1[0:64, 0:NB, :], V0[64:128, 0:NB, :],
                          R=[("V0", 0, b) for b in range(NB)], W=[("V1", 0, b) for b in range(NB)])
                    S.dma("sp", dq["v1"], V1[64:128, 0:NB - 1, :], V0[0:64, 1:NB, :],
                          R=[("V0", 0, b) for b in range(NB)], W=[("V1", 0, b) for b in range(NB)])
                if is_s:
                    state["pool"] = PP_ALL
                yield "ready"
                if DBG.get('stage', 9) < 2:
                    return
                if not is_s:
                    prologue()
                    yield "primed"
                for k in range(nu + 3):
                    if k + 1 < nu:
                        emit_B(units[k + 1], k + 1)
                    if 0 <= k - 2 < nu:
                        emit_F(units[k - 2], k - 2)
                    if k + 1 < nu:
                        emit_C(units[k + 1], k + 1)
                        emit_D(units[k + 1], k + 1)
                    if 0 <= k - 3 < nu:
                        emit_G(units[k - 3], k - 3)
                    if k + 2 < nu:
                        emit_A(units[k + 2], k + 2)
                    if 0 <= k - 1 < nu:
                        emit_E(units[k - 1], k - 1)
                    filler(k, nu)

            if NSET > 1:
                state["pool"] = [(0, 1), (1, 1), (2, 1)]
            projdone = [False] * npairs
            qkdone = [False] * npairs
            gens = []

            def advance(m, stop_at_qk=False):
                if m >= npairs or projdone[m] or (stop_at_qk and qkdone[m]):
                    return
                r = next(gens[m])
                if r == "qk":
                    qkdone[m] = True
                elif r == "ready":
                    projdone[m] = True

            primed = [False] * (npairs + 1)

            def make_filler(m):
                if NSET > 1:
                    def fp(k, nu):
                        if m >= npairs:
                            return
                        if not projdone[m]:
                            advance(m)
                            if k % 2 == 1:
                                advance(m)
                        elif k >= nu - 1 and not primed[m]:
                            primed[m] = True
                            assert next(gens[m]) == "primed"
                    return fp

                def f(k, nu):
                    if k >= nu - 2:
                        advance(m, True)
                        advance(m, True)
                        advance(m, True)
                return f

            for n_ in range(npairs):
                gens.append(pair_body(n_, make_filler(n_ + 1)))
            for n_ in range(npairs):
                while not projdone[n_]:
                    advance(n_)
                for _ in gens[n_]:
                    pass
            state["pool"] = PP_ALL

            out_proj(g, i, "nwout", j, *hook())

        def final_norm_gen(g):
            Xg = X[g.gi]
            y_d = ysT_d if g.gi == 0 else ypT_d
            for ti, (t0, nt) in enumerate(g.tiles):
                t1, k1 = rms_rstd(g, t0, nt, lambda c: ("X", g.gi, c))
                for c in range(NCH):
                    t2, k2 = tmp()
                    S.op("dve", lambda e, c=c, t2=t2: e.scalar_tensor_tensor(
                        out=t2[:, :nt], in0=Xg[:, c, t0:t0 + nt], scalar=PV[:, 4, c:c + 1], in1=t1[:, :nt],
                        op0=ALU.mult, op1=ALU.mult), R=[("X", g.gi, c), "PVEC", k1], W=[k2])
                    S.dma("sp", tsem[k2[1]], y_d[c * 128:(c + 1) * 128, t0:t0 + nt], t2[:, :nt], R=[k2])
                    yield

        def final_norm(g):
            for _ in final_norm_gen(g):
                pass

        groups = [g_ for g_ in [GS, GP] if g_.name in DBG.get('groups', 'SP')]
        both = (len(groups) == 2)

        def run_layer(i, g, hook):
            ada_flush(i)
            with ExitStack() as lst:
                if i % 2 == 0:
                    conv_layer(i, g, lst, hook)
                else:
                    na_layer(i, g, lst, hook)
                S.barrier()

        if depth == 0:
            for g in groups:
                final_norm(g)
        elif not both:
            g = groups[0]
            for i in range(depth):
                modulate(g, i)
                run_layer(i, g, lambda: (None, 0))
            final_norm(g)
        else:
            modulate(GS, 0)
            for i in range(depth):
                run_layer(i, GS, lambda i=i: (modulate_gen(GP, i), 9 * len(GP.tiles)))

                def hook_p(i=i):
                    if i + 1 < depth:
                        ada_flush(i + 1)
                        return modulate_gen(GS, i + 1), 9 * len(GS.tiles)
                    return final_norm_gen(GS), 8 * len(GS.tiles)
                run_layer(i, GP, hook_p)
            final_norm(GP)
        S.barrier()
    return nc, WREC


_NAMES = ["x_prompt", "x_sample", "c", "cache_k", "cache_v", "c_ctx", "norm_g", "ada_w", "ada_b",
          "conv_w_in", "conv_dw_w", "conv_dw_b", "conv_ln_g", "conv_ln_b", "conv_w_out",
          "na_w_in", "na_rpb", "na_w_out", "final_g"]


def _fm(v):
    return np.ascontiguousarray(np.asarray(v, np.float32).reshape(NCH, 128).T)


def _bias_tables(na_rpb):
    qc = np.arange(GW)[:, None]
    kc = np.arange(GW)[None, :]
    qstart = np.clip(qc - 8, 0, GW - 16)
    valid = (kc >= qstart) & (kc < qstart + 16)
    dc = np.clip(kc - qc + 15, 0, 30)
    tm = np.empty((2, 8, 128, 15, GW), np.float32)
    for j in range(2):
        for h in range(16):
            g = na_rpb[j, h][:, dc]
            g = np.where(valid[None], g, np.float32(NEG)).transpose(1, 0, 2)
            tm[j, h // 2, (h % 2) * 64:(h % 2) * 64 + 64] = g
    return np.ascontiguousarray(tm.reshape(2, 8, 128, 15 * GW))


def make_in_maps(inp):
    a = {k: np.asarray(inp[k], dtype=np.float32) for k in _NAMES}
    pv_list = [a["norm_g"][i] for i in range(4)] + [a["final_g"]] + \
              [a["conv_dw_b"][j] for j in range(2)] + [a["conv_ln_g"][j] for j in range(2)] + \
              [a["conv_ln_b"][j] for j in range(2)]
    pv = np.ascontiguousarray(np.stack([_fm(v) for v in pv_list], axis=1))
    adab = np.ascontiguousarray(np.stack([a["ada_b"][i].reshape(24, 128).T for i in range(4)], axis=1))
    dww = np.ascontiguousarray(np.stack(
        [a["conv_dw_w"][j].T.reshape(NCH, 128, CK).transpose(1, 0, 2) for j in range(2)], axis=1))
    tm = _bias_tables(a["na_rpb"])
    ident = np.eye(128, dtype=np.float32)
    cwi = a["conv_w_in"]
    cab = np.ascontiguousarray(np.stack(
        [np.concatenate([cwi[j][:, n * 128:(n + 1) * 128], cwi[j][:, D + n * 128:D + (n + 1) * 128]], axis=1)
         for j in range(2) for n in range(8)], axis=0))
    cz = np.ascontiguousarray(cwi[:, :, 2 * D:3 * D])
    nwi = a["na_w_in"]
    nwp = np.ascontiguousarray(np.stack(
        [np.concatenate([nwi[j][:, part * D + n * 128:part * D + (n + 1) * 128] for part in range(4)], axis=1)
         for j in range(2) for n in range(8)], axis=0))
    dwwr = np.ascontiguousarray(np.roll(dww, -64, axis=0))
    shared = dict(pv=pv, adab=adab, dww=dww, dwwr=dwwr, tm=tm, ident=ident, ada_w=a["ada_w"], cab=cab, cz=cz,
                  conv_w_out=a["conv_w_out"], nwp=nwp, na_w_out=a["na_w_out"])
    maps = []
    for c in range(NCORES):
        b = c // 2
        g0 = 0 if c % 2 == 0 else (32 - LROWS)
        m = dict(shared)
        m["xsT"] = np.ascontiguousarray(a["x_sample"][b, g0 * GW:g0 * GW + TS].T)
        m["xpT"] = np.ascontiguousarray(a["x_prompt"][2 * c:2 * c + 2].reshape(TP, D).T)
        m["cvec"] = np.ascontiguousarray(np.stack([_fm(a["c"][b]), _fm(a["c_ctx"])], axis=2))
        m["ckT"] = np.ascontiguousarray(a["cache_k"][b].reshape(2, PAST, D).transpose(0, 2, 1))
        m["cv"] = np.ascontiguousarray(a["cache_v"][b].reshape(2, PAST, D))
        maps.append(m)
    return maps


def assemble(results):
    y_prompt = np.empty((16, SEQ, D), np.float32)
    y_sample = np.empty((4, 2048, D), np.float32)
    nk = np.empty((16, 2, SEQ, 16, 64), np.float32)
    nvv = np.empty((16, 2, SEQ, 16, 64), np.float32)
    for c in range(NCORES):
        r = results[c]
        b = c // 2
        ys = r["ysT"].T
        if c % 2 == 0:
            y_sample[b, 0:1024] = ys[0:1024]
        else:
            lo = (16 - (32 - LROWS)) * GW
            y_sample[b, 1024:2048] = ys[lo:lo + 1024]
        y_prompt[2 * c:2 * c + 2] = r["ypT"].T.reshape(2, SEQ, D)
        for jl in range(2):
            kt = r["nkT"][jl].T.reshape(2, SEQ, 16, 64)
            vt = r["nv"][jl].reshape(2, SEQ, 16, 64)
            nk[2 * c:2 * c + 2, jl] = kt
            nvv[2 * c:2 * c + 2, jl] = vt
    return y_prompt, y_sample, nk, nvv


_CACHE = {}


def kernel(**inputs):
    if "nc" not in _CACHE:
        _, rec = build_program()
        _CACHE["nc"] = build_program(sched=rec)[0]
    nc = _CACHE["nc"]
    in_maps = make_in_maps(inputs)
    res = run_bass_kernel_spmd(nc, in_maps, core_ids=list(range(NCORES)))
    return assemble(res.results)
```

```python
import numpy as np
from contextlib import ExitStack
import concourse.bass as bass
import concourse.mybir as mybir
from concourse.bass_utils import run_bass_kernel_spmd

F32 = mybir.dt.float32
BF16 = mybir.dt.bfloat16
ALU = mybir.AluOpType
AF = mybir.ActivationFunctionType
AX = mybir.AxisListType

NCORES = 8
DEPTH = 4
D = 1024
NCH = 8
LROWS = 26
GW = 64
TS = LROWS * GW
TP = 512
SEQ = 256
CK = 31
PAST = 512
NEG = -30000.0
EPS = 1e-6
DBG = {}


class Sched:
    def __init__(self, nc, stack):
        self.nc = nc
        self.stack = stack
        self.eng = {"pe": nc.tensor, "act": nc.scalar, "dve": nc.vector, "pool": nc.gpsimd, "sp": nc.sync}
        self.sem = {k: stack.enter_context(nc.semaphore("s_" + k)) for k in self.eng}
        self.cnt = {k: 0 for k in self.eng}
        self.waited = {k: {} for k in self.eng}
        self.last_w = {}
        self.readers = {}
        self.dma_cnt = {}
        self.dma_keys = []

    def _wait(self, e, semkey, val):
        kid = semkey if isinstance(semkey, str) else semkey[2]
        if e == "pe" and semkey == "pe":
            return
        if self.waited[e].get(kid, 0) >= val:
            return
        self.waited[e][kid] = val
        sem = self.sem[semkey] if isinstance(semkey, str) else semkey[1]
        self.eng[e].wait_ge(sem, val)

    def _deps(self, e, R, W, relaxed=False):
        for r in R:
            lw = self.last_w.get(r)
            if lw is not None:
                self._wait(e, lw[0], lw[1])
            if isinstance(r, tuple) and r[0] == "PS":
                for rd in self.readers.get(r, ()):
                    if rd[0] != e:
                        self._wait(e, rd[0], rd[1])
        for w in W:
            lw = self.last_w.get(w)
            if lw is not None and not (relaxed and lw[0] == e):
                self._wait(e, lw[0], lw[1])
            for rd in self.readers.get(w, ()):
                if not (relaxed and rd[0] == e):
                    self._wait(e, rd[0], rd[1])

    def _commit(self, tok, R, W):
        for r in R:
            self.readers.setdefault(r, []).append(tok)
        for w in W:
            self.last_w[w] = tok
            self.readers[w] = []

    def op(self, e, fn, R=(), W=()):
        self._deps(e, R, W)
        ins = fn(self.eng[e])
        self.cnt[e] += 1
        ins.then_inc(self.sem[e], 1)
        self._commit((e, self.cnt[e]), R, W)
        return ins

    def new_dma_sem(self, name):
        s = self.stack.enter_context(self.nc.semaphore(name))
        key = ("dma", s, name)
        self.dma_cnt[name] = 0
        self.dma_keys.append(key)
        return key

    def dma(self, q, semkey, out, in_, R=(), W=()):
        self._deps(q, R, W)
        ins = self.eng[q].dma_start(out=out, in_=in_)
        self.dma_cnt[semkey[2]] += 16
        ins.then_inc(semkey[1], 16)
        self._commit((semkey, self.dma_cnt[semkey[2]]), R, W)
        return ins

    def barrier(self):
        for e in self.eng:
            for k in self.eng:
                if k != e and self.cnt[k]:
                    self._wait(e, k, self.cnt[k])
            for key in self.dma_keys:
                if self.dma_cnt[key[2]]:
                    self._wait(e, key, self.dma_cnt[key[2]])
        self.last_w = {}
        self.readers = {}


class Grp:
    def __init__(self, name, gi, T, v, seqs, tile):
        self.name = name
        self.gi = gi
        self.T = T
        self.v = v
        self.seqs = seqs
        self.tiles = []
        for (s0, sl) in seqs:
            t = s0
            while t < s0 + sl:
                n = min(tile, s0 + sl - t)
                self.tiles.append((t, n))
                t += n

    def tidx(self, t):
        for i, (t0, n) in enumerate(self.tiles):
            if t0 <= t < t0 + n:
                return i
        raise ValueError(t)

    def tis(self, t0, n):
        return sorted({self.tidx(t0), self.tidx(t0 + n - 1)})

    def uoff(self, t):
        for si, (s0, sl) in enumerate(self.seqs):
            if s0 <= t < s0 + sl:
                return 15 + t + 15 * si
        raise ValueError(t)

    def ulen(self):
        return self.T + 15 * (len(self.seqs) + 1)


def build_program(depth=DEPTH, sched=None):
    nc = bass.Bass("TRN2", target_bir_lowering=False)
    WREC = []

    def din(name, shape):
        return nc.dram_tensor(name, list(shape), F32, kind="ExternalInput").ap()

    def dout(name, shape):
        return nc.dram_tensor(name, list(shape), F32, kind="ExternalOutput").ap()

    xsT_d = din("xsT", [D, TS])
    xpT_d = din("xpT", [D, TP])
    cvec_d = din("cvec", [128, NCH, 2])
    pv_d = din("pv", [128, 11, NCH])
    adab_d = din("adab", [128, 4, 24])
    dww_d = din("dww", [128, 2, NCH, CK])
    dwwr_d = din("dwwr", [128, 2, NCH, CK])
    ckT_d = din("ckT", [2, D, PAST])
    cv_d = din("cv", [2, PAST, D])
    tm_d = din("tm", [2, 8, 128, 960])
    ident_d = din("ident", [128, 128])
    ada_w_d = din("ada_w", [4, D, 3 * D])
    cab_d = din("cab", [16, D, 256])
    cz_d = din("cz", [2, D, D])
    cwout_d = din("conv_w_out", [2, D, D])
    nwp_d = din("nwp", [16, D, 512])
    nwout_d = din("na_w_out", [2, D, D])

    ysT_d = dout("ysT", [D, TS])
    ypT_d = dout("ypT", [D, TP])
    nkT_d = dout("nkT", [2, D, TP])
    nv_d = dout("nv", [2, TP, D])

    GS = Grp("S", 0, TS, 0, [(0, TS)], 512)
    GP = Grp("P", 1, TP, 1, [(0, SEQ), (SEQ, SEQ)], 256)

    with ExitStack() as st:
        S = Sched(nc, st)

        def sb(name, shape, dt, stack=st):
            return stack.enter_context(nc.sbuf_tensor(name, list(shape), dt))

        XS = sb("XS", [128, NCH, TS], F32)
        XP = sb("XP", [128, NCH, TP], F32)
        X = {0: XS, 1: XP}
        HT = sb("HT", [128, NCH, TS], BF16)
        GT = sb("GT", [128, NCH, TS], BF16)
        NWB = 3
        WB = [sb("WB%d" % k, [128, NCH, 512], BF16) for k in range(NWB)]
        SQR = [sb("SQ%d" % k, [128, 512], BF16) for k in range(2)]
        HTP = sb("HTP", [128, NCH, TP], BF16)
        HTG = {0: HT, 1: HTP}
        NT = 5
        TT = [sb("T%d" % k, [128, 512], F32) for k in range(NT)]
        TR = [sb("TR%d" % k, [128, 512], F32) for k in range(2)]
        IDB = sb("IDB", [128, 128], BF16)
        ONES = sb("ONES", [128, 128], BF16)
        EPSB = sb("EPSB", [128, 1], F32)
        NEG1 = sb("NEG1", [128, 1], F32)
        CV = sb("CV", [128, NCH, 2], F32)
        SCT = sb("SCT", [128, NCH, 2], BF16)
        PV = sb("PVEC", [128, 11, NCH], F32)
        ADAB = sb("ADAB", [128, 4, 24], F32)
        DWW = sb("DWW", [128, 2, NCH, CK], BF16)
        DWWR = sb("DWWR", [128, 2, NCH, CK], BF16)
        ID64 = sb("ID64", [128, 64], BF16)
        MODS = sb("MODS", [128, 4, 2, 24], F32)
        AMOD = sb("AMOD", [128, 4, 2, NCH], F32)
        GATE = sb("GATE", [128, 4, 2, NCH], F32)
        MSTG = sb("MSTG", [128, 48], F32)

        PS = [st.enter_context(nc.psum_tensor("PS%d" % k, [128, 1024], F32)) for k in range(4)]

        dq = {k: S.new_dma_sem("dq_" + k) for k in ("ld", "ld2", "w0", "w1", "w2", "aux", "idb", "v1", "u1", "u1b", "kc0", "kc1", "vc0", "vc1", "tm0", "tm1", "ko0", "ko1")}
        wsem = [dq["w0"], dq["w1"], dq["w2"]]
        tsem = [S.new_dma_sem("dq_t%d" % k) for k in range(NT)]

        state = {"t": 0, "wb": 0, "pp": 0, "sp": 0, "tr": 0}

        def tmp():
            k = state["t"] % NT
            state["t"] += 1
            return TT[k], ("T", k)

        def tmpr():
            k = state["tr"] % 2
            state["tr"] += 1
            return TR[k], ("TR", k)

        DR = {"ada_w": ada_w_d, "cab": cab_d, "cz": cz_d, "cwout": cwout_d, "nwp": nwp_d, "nwout": nwout_d}
        wst = {"idx": 0, "issued": 0}

        def w_issue(idx, spec):
            k = idx % NWB
            for (dlo, width, wname, li, slo) in spec:
                S.dma("pool", wsem[k], WB[k][:, :, dlo:dlo + width], wview(DR[wname][li])[:, :, slo:slo + width], W=[("WB", k)])

        def wnext(spec):
            idx = wst["idx"]
            wst["idx"] += 1
            if sched is None:
                WREC.append(spec)
                w_issue(idx, spec)
            else:
                assert sched[idx] == spec, (idx, sched[idx], spec)
                while wst["issued"] <= min(idx + 2, len(sched) - 1):
                    w_issue(wst["issued"], sched[wst["issued"]])
                    wst["issued"] += 1
            return WB[idx % NWB], ("WB", idx % NWB)

        PP_ALL = [(0, 0), (0, 1), (1, 0), (1, 1)]
        state["pool"] = PP_ALL

        def projps():
            pool_ = state["pool"]
            t, h = pool_[state["pp"] % len(pool_)]
            state["pp"] += 1
            return PS[t][:, h * 512:h * 512 + 512], ("PS", t, h)

        SP_ALL = [(2, 0), (2, 1)]
        state["spool"] = SP_ALL

        def statps():
            pool_ = state["spool"]
            t, h = pool_[state["sp"] % len(pool_)]
            state["sp"] += 1
            return PS[t][:, h * 512:h * 512 + 512], ("PS", t, h)

        def wview(w2d):
            return w2d.rearrange("(c p) n -> p c n", p=128)

        S.dma("sp", dq["aux"], CV[:], cvec_d, W=["CV"])
        S.dma("sp", dq["aux"], PV[:], pv_d, W=["PVEC"])
        S.dma("sp", dq["aux"], ADAB[:], adab_d, W=["ADAB"])
        S.dma("pool", dq["idb"], DWW[:], dww_d, W=["DWW"])
        S.dma("pool", dq["idb"], DWWR[:], dwwr_d, W=["DWWR"])
        S.dma("pool", dq["idb"], IDB[:], ident_d, W=["IDB"])
        S.op("dve", lambda e: e.memset(ONES[:], 1.0 / D), W=["ONES"])
        S.op("dve", lambda e: e.memset(EPSB[:], EPS), W=["EPSB"])
        S.op("dve", lambda e: e.memset(NEG1[:], -1.0), W=["NEG1"])
        S.barrier()
        S.dma("sp", dq["ld"], XS[:], xsT_d.rearrange("(c p) t -> p c t", p=128),
              W=[("X", 0, c) for c in range(NCH)])
        S.dma("sp", dq["ld2"], XP[:], xpT_d.rearrange("(c p) t -> p c t", p=128),
              W=[("X", 1, c) for c in range(NCH)])
        S.op("act", lambda e: e.activation(out=SCT[:], in_=CV[:], func=AF.Silu), R=["CV"], W=["SCT"])
        S.op("dve", lambda e: e.tensor_tensor(out=ID64[:], in0=IDB[:, 0:64], in1=IDB[:, 64:128], op=ALU.add), R=["IDB"], W=["ID64"])

        ada_q = [(i, blk) for i in range(depth) for blk in range(6)]

        def ada_step():
            if not ada_q:
                return
            i, blk = ada_q.pop(0)
            wb, wk = wnext([(0, 512, "ada_w", i, blk * 512)])
            aps = PS[3][:, 0:8]
            for jn in range(4):
                for c in range(NCH):
                    S.op("pe", lambda e, c=c, jn=jn, wb=wb: e.matmul(
                        aps[:, 2 * jn:2 * jn + 2], lhsT=wb[:, c, jn * 128:(jn + 1) * 128], rhs=SCT[:, c, :],
                        start=(c == 0), stop=(c == NCH - 1)),
                        R=[wk, "SCT"], W=[("PS", 3, 0)])
            S.op("dve", lambda e: e.tensor_copy(out=MSTG[:, 8 * blk:8 * blk + 8], in_=aps), R=[("PS", 3, 0)], W=["MSTG"])
            if blk == 5:
                m3 = MSTG[:].rearrange("p (n v) -> p n v", v=2)
                for v in range(2):
                    S.op("dve", lambda e, v=v: e.tensor_tensor(out=MODS[:, i, v, :], in0=m3[:, :, v], in1=ADAB[:, i, :], op=ALU.add),
                         R=["MSTG", "ADAB"], W=[("MODS", i)])
                    S.op("dve", lambda e, v=v: e.scalar_tensor_tensor(
                        out=AMOD[:, i, v, :], in0=MODS[:, i, v, 8:16], scalar=1.0, in1=PV[:, i, :],
                        op0=ALU.add, op1=ALU.mult), R=[("MODS", i), "PVEC"], W=[("AMOD", i)])
                    S.op("dve", lambda e, v=v: e.tensor_scalar(out=GATE[:, i, v, :], in0=MODS[:, i, v, 16:24],
                                                               scalar1=(0.5 if i % 2 == 1 else 1.0), scalar2=None, op0=ALU.mult),
                         R=[("MODS", i)], W=[("GATE", i)])

        def ada_flush(upto):
            while ada_q and ada_q[0][0] <= upto:
                ada_step()

        ada_flush(0)

        def rms_rstd(g, t0, nt, xr):
            Xg = X[g.gi]
            ps, pk = statps()
            for c in range(NCH):
                sq = SQR[c % 2]
                S.op("act", lambda e, c=c, sq=sq: e.activation(out=sq[:, :nt], in_=Xg[:, c, t0:t0 + nt], func=AF.Square),
                     R=[xr(c)], W=[("SQ", c % 2)])
                S.op("pe", lambda e, c=c, sq=sq: e.matmul(ps[:, :nt], lhsT=ONES[:], rhs=sq[:, :nt],
                                                          start=(c == 0), stop=(c == NCH - 1)),
                     R=["ONES", ("SQ", c % 2)], W=[pk])
            t1, k1 = tmpr()
            S.op("act", lambda e: e.activation(out=t1[:, :nt], in_=ps[:, :nt], func=AF.Sqrt, bias=EPSB[:, 0:1], scale=1.0),
                 R=[pk, "EPSB"], W=[k1])
            S.op("dve", lambda e: e.reciprocal(out=t1[:, :nt], in_=t1[:, :nt]), R=[k1], W=[k1])
            return t1, k1

        def modulate_gen(g, i):
            Xg = X[g.gi]
            for ti, (t0, nt) in enumerate(g.tiles):
                t1, k1 = rms_rstd(g, t0, nt, lambda c: ("X", g.gi, c))
                yield
                for c in range(NCH):
                    t2, k2 = tmp()
                    S.op("dve", lambda e, c=c, t2=t2: e.scalar_tensor_tensor(
                        out=t2[:, :nt], in0=Xg[:, c, t0:t0 + nt], scalar=AMOD[:, i, g.v, c:c + 1], in1=t1[:, :nt],
                        op0=ALU.mult, op1=ALU.mult), R=[("X", g.gi, c), ("AMOD", i), k1], W=[k2])
                    S.op("act", lambda e, c=c, t2=t2: e.activation(
                        out=HTG[g.gi][:, c, t0:t0 + nt], in_=t2[:, :nt], func=AF.Identity,
                        bias=MODS[:, i, g.v, c:c + 1], scale=1.0), R=[k2, ("MODS", i)], W=[("HT", g.gi, c, ti)])
                    yield

        def modulate(g, i):
            for _ in modulate_gen(g, i):
                pass

        def proj_mm(wb, wk, col0, g, ti):
            t0, nt = g.tiles[ti]
            ps, pk = projps()
            for c in range(NCH):
                S.op("pe", lambda e, c=c: e.matmul(ps[:, :nt], lhsT=wb[:, c, col0:col0 + 128], rhs=HTG[g.gi][:, c, t0:t0 + nt],
                                                    start=(c == 0), stop=(c == NCH - 1)),
                     R=[wk, ("HT", g.gi, c, ti)], W=[pk])
            return ps, pk, t0, nt, ti

        def proj_fm(wb, wk, col0, g, ti, evac):
            evac(*proj_mm(wb, wk, col0, g, ti))

        def out_proj(g, i, wname, j, bg=None, bg_steps=0):
            Xg = X[g.gi]
            ngroups = NCH * len(g.tiles)
            per = -(-bg_steps // ngroups) if bg is not None else 0
            for blk in range(2):
                wb, wk = wnext([(0, 512, wname, j, blk * 512)])
                for jm in range(4):
                    m = blk * 4 + jm
                    for ti, (t0, nt) in enumerate(g.tiles):
                        ps, pk = projps()
                        for n in range(NCH):
                            S.op("pe", lambda e, n=n: e.matmul(ps[:, :nt], lhsT=wb[:, n, jm * 128:(jm + 1) * 128],
                                                                rhs=GT[:, n, t0:t0 + nt], start=(n == 0), stop=(n == NCH - 1)),
                                 R=[wk, ("GT", n, ti)], W=[pk])
                        S.op("dve", lambda e: e.scalar_tensor_tensor(
                            out=Xg[:, m, t0:t0 + nt], in0=ps[:, :nt], scalar=GATE[:, i, g.v, m:m + 1],
                            in1=Xg[:, m, t0:t0 + nt], op0=ALU.mult, op1=ALU.add),
                            R=[pk, ("GATE", i), ("X", g.gi, m)], W=[("X", g.gi, m)])
                        for _ in range(per):
                            if bg is not None and next(bg, "end") == "end":
                                bg = None
            if bg is not None:
                for _ in bg:
                    pass

        def conv_layer(i, g, lst, hook):
            j = i // 2
            UL = g.ulen()
            U = [sb("U%d_%d_%d" % (i, g.gi, k), [128, UL], BF16, lst) for k in range(2)]
            NPB = 2 if g.gi == 1 else 1
            U1s = [sb("U1_%d_%d_%d" % (i, g.gi, k), [128, UL], BF16, lst) for k in range(NPB)]
            DGAs = [sb("DGA%d_%d_%d" % (i, g.gi, k), [128, CK, 64], BF16, lst) for k in range(NPB)]
            DGBs = [sb("DGB%d_%d_%d" % (i, g.gi, k), [128, CK, 64], BF16, lst) for k in range(NPB)]
            NS0 = (CK + 1) // 2
            MR = sb("MR%d_%d" % (i, g.gi), [128, 2, g.T], F32, lst)
            if DBG.get('mem'): print('conv mem remaining', g.name, nc.sbuf_bytes_remaining)
            for k in range(2):
                S.op("pool", lambda e, k=k: e.memset(U[k][:], 0.0), W=[("U", k)])
            def glu(n):
                wb, wk = wnext([(0, 256, "cab", j * 8 + n, 0)])
                ub = n % 2
                for ti, (t0, nt) in enumerate(g.tiles):
                    pa, ka = projps()
                    pb, kb = projps()
                    for (ps, pk, col0) in ((pa, ka, 0), (pb, kb, 128)):
                        for c in range(NCH):
                            S.op("pe", lambda e, c=c, ps=ps, col0=col0: e.matmul(
                                ps[:, :nt], lhsT=wb[:, c, col0:col0 + 128], rhs=HTG[g.gi][:, c, t0:t0 + nt],
                                start=(c == 0), stop=(c == NCH - 1)), R=[wk, ("HT", g.gi, c, ti)], W=[pk])
                    t1, k1 = tmp()
                    S.op("act", lambda e: e.activation(out=t1[:, :nt], in_=pb[:, :nt], func=AF.Sigmoid), R=[kb], W=[k1])
                    uo = g.uoff(t0)
                    S.op("dve", lambda e: e.tensor_tensor(out=U[ub][:, uo:uo + nt], in0=pa[:, :nt], in1=t1[:, :nt], op=ALU.mult),
                         R=[ka, k1], W=[("U", ub)])

            def prep(n):
                ub = n % 2
                pb_ = n % NPB
                U1, DGA, DGB = U1s[pb_], DGAs[pb_], DGBs[pb_]
                S.op("dve", lambda e: e.tensor_tensor(
                    out=DGA[:], in0=ID64[:].unsqueeze(1).to_broadcast([128, CK, 64]),
                    in1=DWW[:, j, n, :].unsqueeze(2).to_broadcast([128, CK, 64]), op=ALU.mult),
                    R=["ID64", "DWW"], W=[("DGA", pb_)])
                S.op("dve", lambda e: e.tensor_tensor(
                    out=DGB[:], in0=ID64[:].unsqueeze(1).to_broadcast([128, CK, 64]),
                    in1=DWWR[:, j, n, :].unsqueeze(2).to_broadcast([128, CK, 64]), op=ALU.mult),
                    R=["ID64", "DWWR"], W=[("DGB", pb_)])
                S.dma("sp", dq["u1b" if pb_ else "u1"], U1[0:64, :], U[ub][64:128, :], R=[("U", ub)], W=[("U1", pb_)])
                S.dma("sp", dq["u1b" if pb_ else "u1"], U1[64:128, :], U[ub][0:64, :], R=[("U", ub)], W=[("U1", pb_)])

            def conv(n):
                ub = n % 2
                for ti, (t0, nt) in enumerate(g.tiles):
                    pt_ = 2 + (ti % 2)
                    banks = [(PS[pt_][:, 0:512], ("PS", pt_, 0)), (PS[pt_][:, 512:1024], ("PS", pt_, 1))]
                    uo = g.uoff(t0)
                    for sl in range(NS0):
                        for (ih, jh) in ((0, 0), (1, 1), (1, 0), (0, 1)):
                            own = (ih == jh)
                            k = sl if own else NS0 + sl
                            if k >= CK:
                                continue
                            first = (sl == 0)
                            last = (k == NS0 - 1) if own else (k == CK - 1)
                            src, skey = (U[ub], ("U", ub)) if own else (U1s[n % NPB], ("U1", n % NPB))
                            dg, dkey = (DGAs[n % NPB], ("DGA", n % NPB)) if own else (DGBs[n % NPB], ("DGB", n % NPB))
                            bk, bkey = banks[ih]
                            S.op("pe", lambda e, ih=ih, jh=jh, k=k, src=src, dg=dg, bk=bk, first=first, last=last: e.matmul(
                                bk[64 * jh:64 * jh + 64, :nt], lhsT=dg[64 * ih:64 * ih + 64, k, :],
                                rhs=src[64 * ih:64 * ih + 64, uo + k - 15:uo + k - 15 + nt],
                                start=first, stop=last, tile_position=((64 * ih, 64 * jh) if (ih or jh) else None)),
                                R=[dkey, skey], W=[bkey])
                    t1, k1 = tmp()
                    t2, k2 = tmp()
                    S.op("act", lambda e: e.activation(out=t1[:, :nt], in_=banks[0][0][:, :nt], func=AF.Identity,
                                                       bias=PV[:, 5 + j, n:n + 1], scale=1.0),
                         R=[banks[0][1], "PVEC"], W=[k1])
                    S.op("dve", lambda e: e.tensor_copy(out=t2[:, :nt], in_=banks[1][0][:, :nt]), R=[banks[1][1]], W=[k2])
                    S.op("pool", lambda e: e.tensor_tensor(out=GT[:, n, t0:t0 + nt], in0=t1[:, :nt], in1=t2[:, :nt], op=ALU.add),
                         R=[k1, k2], W=[("GT", n, ti)])

            bg0, bg_total = hook()
            bgs = [bg0]
            bg_per = -(-bg_total * 2 // (3 * NCH)) if bg0 is not None else 0
            glu(0)
            prep(0)
            for n in range(NCH):
                if n + 1 < NCH:
                    glu(n + 1)
                    if NPB > 1:
                        prep(n + 1)
                conv(n)
                if n + 1 < NCH and NPB == 1:
                    prep(n + 1)
                ada_step()
                for _ in range(bg_per):
                    if bgs[0] is not None and next(bgs[0], "end") == "end":
                        bgs[0] = None

            state["spool"] = SP_ALL
            for ti, (t0, nt) in enumerate(g.tiles):
                pm, km = statps()
                pq, kq = statps()
                for c in range(NCH):
                    S.op("pe", lambda e, c=c: e.matmul(pm[:, :nt], lhsT=ONES[:], rhs=GT[:, c, t0:t0 + nt],
                                                        start=(c == 0), stop=(c == NCH - 1)), R=["ONES", ("GT", c, ti)], W=[km])
                for c in range(NCH):
                    sq = SQR[c % 2]
                    S.op("act", lambda e, c=c, sq=sq: e.activation(out=sq[:, :nt], in_=GT[:, c, t0:t0 + nt], func=AF.Square),
                         R=[("GT", c, ti)], W=[("SQ", c % 2)])
                    S.op("pe", lambda e, c=c, sq=sq: e.matmul(pq[:, :nt], lhsT=ONES[:], rhs=sq[:, :nt],
                                                              start=(c == 0), stop=(c == NCH - 1)), R=["ONES", ("SQ", c % 2)], W=[kq])
                S.op("act", lambda e: e.activation(out=MR[:, 0, t0:t0 + nt], in_=pm[:, :nt], func=AF.Copy), R=[km], W=[("MR", 0, ti)])
                t1, k1 = tmp()
                S.op("dve", lambda e: e.tensor_tensor(out=t1[:, :nt], in0=MR[:, 0, t0:t0 + nt], in1=MR[:, 0, t0:t0 + nt], op=ALU.mult),
                     R=[("MR", 0, ti)], W=[k1])
                S.op("dve", lambda e: e.tensor_tensor(out=t1[:, :nt], in0=pq[:, :nt], in1=t1[:, :nt], op=ALU.subtract),
                     R=[kq, k1], W=[k1])
                S.op("act", lambda e: e.activation(out=MR[:, 1, t0:t0 + nt], in_=t1[:, :nt], func=AF.Sqrt, bias=EPSB[:, 0:1], scale=1.0),
                     R=[k1, "EPSB"], W=[("MR", 1, ti)])
                S.op("dve", lambda e: e.reciprocal(out=MR[:, 1, t0:t0 + nt], in_=MR[:, 1, t0:t0 + nt]),
                     R=[("MR", 1, ti)], W=[("MR", 1, ti)])

            for blk in range(2):
                wb, wk = wnext([(0, 512, "cz", j, blk * 512)])
                for jn in range(4):
                    n = blk * 4 + jn
                    for ti in range(len(g.tiles)):
                        def evac(ps, pk, t0, nt, ti, n=n):
                            t1, k1 = tmp()
                            S.op("act", lambda e: e.activation(out=t1[:, :nt], in_=ps[:, :nt], func=AF.Silu), R=[pk], W=[k1])
                            t2, k2 = tmp()
                            S.op("dve", lambda e: e.tensor_tensor(out=t2[:, :nt], in0=GT[:, n, t0:t0 + nt], in1=MR[:, 0, t0:t0 + nt], op=ALU.subtract),
                                 R=[("GT", n, ti), ("MR", 0, ti)], W=[k2])
                            S.op("dve", lambda e: e.tensor_tensor(out=t2[:, :nt], in0=t2[:, :nt], in1=MR[:, 1, t0:t0 + nt], op=ALU.mult),
                                 R=[k2, ("MR", 1, ti)], W=[k2])
                            t3, k3 = tmp()
                            S.op("act", lambda e: e.activation(out=t3[:, :nt], in_=t2[:, :nt], func=AF.Silu,
                                                               bias=PV[:, 9 + j, n:n + 1], scale=PV[:, 7 + j, n:n + 1]),
                                 R=[k2, "PVEC"], W=[k3])
                            S.op("dve", lambda e: e.tensor_tensor(out=GT[:, n, t0:t0 + nt], in0=t3[:, :nt], in1=t1[:, :nt], op=ALU.mult),
                                 R=[k3, k1], W=[("GT", n, ti)])
                        proj_fm(wb, wk, jn * 128, g, ti, evac)

            out_proj(g, i, "cwout", j, bgs[0], bg_total // 3)

        def na_layer(i, g, lst, hook):
            j = i // 2
            is_s = (g.gi == 0)
            T = g.T
            NB = (T + 127) // 128
            NSET = 1 if is_s else 2
            QTs = [sb("QT%d_%d_%d" % (i, g.gi, k), [128, T], BF16, lst) for k in range(NSET)]
            KTs = [sb("KT%d_%d_%d" % (i, g.gi, k), [128, T], BF16, lst) for k in range(NSET)]
            SZs = [sb("SZ%d_%d_%d" % (i, g.gi, k), [128, T], BF16, lst) for k in range(NSET)]
            V0s = [sb("V0%d_%d_%d" % (i, g.gi, k), [128, NB, 128], BF16, lst) for k in range(NSET)]
            V1 = sb("V1%d_%d" % (i, g.gi), [128, NB, 128], BF16, lst) if is_s else None
            if is_s:
                KC2 = [sb("KC%d_%d" % (i, k), [128, PAST], BF16, lst) for k in range(2)]
                VC2 = [sb("VC%d_%d" % (i, k), [128, PAST // 128, 128], BF16, lst) for k in range(2)]
                TM2 = [sb("TM%d_%d" % (i, k), [128, 960], BF16, lst) for k in range(2)]

                def load_ctx(n):
                    k = n % 2
                    S.dma("pool", dq["kc%d" % k], KC2[k][:], ckT_d[j, n * 128:(n + 1) * 128, :], W=[("KC", k)])
                    S.dma("pool", dq["vc%d" % k], VC2[k][:], cv_d[j].rearrange("(b p) f -> p b f", p=128)[:, :, n * 128:(n + 1) * 128], W=[("VC", k)])
                    S.dma("pool", dq["tm%d" % k], TM2[k][:], tm_d[j, n], W=[("TM", k)])
            PB_ = [sb("P%d_%d_%d" % (i, g.gi, k), [128, 1024], BF16, lst) for k in range(3)]
            PTB = [sb("PT%d_%d_%d" % (i, g.gi, k), [128, 1024], BF16, lst) for k in range(2)]
            DD = [sb("DD%d_%d_%d" % (i, g.gi, k), [128, 128], BF16, lst) for k in range(3)]
            STT = [sb("ST%d_%d_%d" % (i, g.gi, k), [128, 4], F32, lst) for k in range(2)]
            KO = [sb("KO%d_%d_%d" % (i, g.gi, k), [128, 512], F32, lst) for k in range(2)] if not is_s else None

            if DBG.get('mem'): print('NA mem remaining', g.name, nc.sbuf_bytes_remaining)
            kos = {"k": 0}

            npairs = DBG.get('pairs', NCH)

            def pair_body(n, filler):
                sx = n % NSET
                QT, KT, SZ, V0 = QTs[sx], KTs[sx], SZs[sx], V0s[sx]
                wb, wk = wnext([(0, 512, "nwp", j * 8 + n, 0)])
                if is_s:
                    if n == 0:
                        load_ctx(0)
                    KC, VC, TM = KC2[n % 2], VC2[n % 2], TM2[n % 2]
                    KCk, VCk, TMk = ("KC", n % 2), ("VC", n % 2), ("TM", n % 2)

                def evq(ps, pk, t0, nt, ti):
                    S.op("act", lambda e: e.activation(out=QT[:, t0:t0 + nt], in_=ps[:, :nt], func=AF.Identity, scale=0.125),
                         R=[pk], W=[("QT", sx, ti)])

                def evk(ps, pk, t0, nt, ti):
                    S.op("dve", lambda e: e.tensor_copy(out=KT[:, t0:t0 + nt], in_=ps[:, :nt]), R=[pk], W=[("KT", sx, ti)])
                    if not is_s and 'ko' not in DBG.get('skip', ''):
                        kb = kos["k"] % 2
                        kos["k"] += 1
                        S.op("act", lambda e: e.activation(out=KO[kb][:, :nt], in_=ps[:, :nt], func=AF.Copy), R=[pk], W=[("KO", kb)])
                        S.dma("sp", dq["ko%d" % kb], nkT_d[j, n * 128:(n + 1) * 128, t0:t0 + nt], KO[kb][:, :nt], R=[("KO", kb)])

                def evz(ps, pk, t0, nt, ti):
                    t1, k1 = tmp()
                    S.op("act", lambda e: e.activation(out=t1[:, :nt], in_=ps[:, :nt], func=AF.Tanh, scale=0.5), R=[pk], W=[k1])
                    S.op("dve", lambda e: e.scalar_tensor_tensor(out=SZ[:, t0:t0 + nt], in0=t1[:, :nt], scalar=1.0, in1=ps[:, :nt],
                                                                 op0=ALU.add, op1=ALU.mult), R=[k1, pk], W=[("SZ", sx, ti)])

                for (col0, ev) in ((0, evq), (128, evk)):
                    for ti in range(len(g.tiles)):
                        r_ = proj_mm(wb, wk, col0, g, ti)
                        yield "p"
                        ev(*r_)
                yield "qk"
                if is_s and n + 1 < NCH:
                    load_ctx(n + 1)
                units = []
                if is_s:
                    for r in range(LROWS):
                        rs = min(max(r - 4, 0), LROWS - 8)
                        vv, vn = (V0, "V0") if rs % 2 == 0 else (V1, "V1")
                        vb0 = rs // 2
                        units.append(dict(q0=r * GW, k0=rs * GW, nkl=512, tmo=(rs - r + 7) * GW, ctx=True,
                                          vv=vv, vn=vn, vblocks=[vb0 + x for x in range(4)]))
                else:
                    for p in range(2):
                        for qb in range(SEQ // 64):
                            units.append(dict(q0=p * SEQ + qb * 64, k0=p * SEQ, nkl=SEQ, tmo=None, ctx=False,
                                              vv=V0, vn="V0", vblocks=[2 * p, 2 * p + 1]))

                def emit_A(u, ui):
                    sp_ = PS[ui % 2]
                    sk = [("PS", ui % 2, 0), ("PS", ui % 2, 1)]
                    q0, k0, nkl = u["q0"], u["k0"], u["nkl"]
                    qti = g.tis(q0, 64)
                    kti = g.tis(k0, nkl)
                    hbs = [(0, 64, None), (64, 128, (64, 64))]
                    for (lo, hi, tp) in hbs:
                        S.op("pe", lambda e, lo=lo, hi=hi, tp=tp: e.matmul(
                            sp_[lo:hi, 0:nkl], lhsT=QT[lo:hi, q0:q0 + 64], rhs=KT[lo:hi, k0:k0 + nkl],
                            start=True, stop=(u["tmo"] is None), tile_position=tp),
                            R=[("QT", sx, x) for x in qti] + [("KT", sx, x) for x in kti], W=[sk[0]])
                    if u["tmo"] is not None:
                        for (lo, hi, tp) in hbs:
                            S.op("pe", lambda e, lo=lo, hi=hi, tp=tp: e.matmul(
                                sp_[lo:hi, 0:nkl], lhsT=IDB[lo:hi, lo:hi], rhs=TM[lo:hi, u["tmo"]:u["tmo"] + nkl],
                                start=False, stop=True, tile_position=tp), R=["IDB", TMk], W=[sk[0]])
                    if u["ctx"]:
                        for (lo, hi, tp) in hbs:
                            S.op("pe", lambda e, lo=lo, hi=hi, tp=tp: e.matmul(
                                sp_[lo:hi, 512:512 + PAST], lhsT=QT[lo:hi, q0:q0 + 64], rhs=KC[lo:hi, :],
                                start=True, stop=True, tile_position=tp),
                                R=[("QT", sx, x) for x in qti] + [KCk], W=[sk[1]])

                def _sm(u, ui):
                    sp_ = PS[ui % 2]
                    sk = [("PS", ui % 2, 0), ("PS", ui % 2, 1)]
                    nk = 1024 if u["ctx"] else u["nkl"]
                    rsk = sk if u["ctx"] else sk[:1]
                    return sp_, nk, rsk, STT[ui % 2], ("ST", ui % 2), PB_[ui % 3], ("P", ui % 3)

                def emit_B(u, ui):
                    sp_, nk, rsk, stt, stk, P_, pk_ = _sm(u, ui)
                    S.op("dve", lambda e: e.reduce_max(out=stt[:, 0:1], in_=sp_[:, 0:nk], axis=AX.X, negate=True), R=rsk, W=[stk])

                def emit_C(u, ui):
                    sp_, nk, rsk, stt, stk, P_, pk_ = _sm(u, ui)
                    S.op("act", lambda e: e.activation(out=P_[:, 0:nk], in_=sp_[:, 0:nk], func=AF.Exp, bias=stt[:, 0:1], scale=1.0,
                                                       accum_out=stt[:, 2:3]), R=rsk + [stk], W=[pk_, ("SM", ui % 2)])

                def emit_D(u, ui):
                    sp_, nk, rsk, stt, stk, P_, pk_ = _sm(u, ui)
                    smk = ("SM", ui % 2)
                    S.op("pool", lambda e: e.tensor_tensor(out=stt[:, 3:4], in0=stt[:, 2:3], in1=NEG1[:], op=ALU.pow),
                         R=[smk, "NEG1"], W=[("RS", ui % 2)])
                    S.op("pool", lambda e: e.tensor_tensor(out=DD[ui % 3][:], in0=IDB[:], in1=stt[:, 3:4].to_broadcast([128, 128]), op=ALU.mult),
                         R=["IDB", ("RS", ui % 2)], W=[("DD", ui % 3)])

                def emit_E(u, ui):
                    nk = 1024 if u["ctx"] else u["nkl"]
                    P_, pk_ = PB_[ui % 3], ("P", ui % 3)
                    for jj in range(nk // 128):
                        S.op("pe", lambda e, jj=jj: e.matmul(PS[2][:, jj * 128:(jj + 1) * 128], lhsT=P_[:, jj * 128:(jj + 1) * 128],
                                                              rhs=DD[ui % 3][:], start=True, stop=True),
                             R=[pk_, ("DD", ui % 3)], W=[("PS", 2, jj // 4)])

                def emit_F(u, ui):
                    nk = 1024 if u["ctx"] else u["nkl"]
                    PT_, ptk = PTB[ui % 2], ("PT", ui % 2)
                    h0 = min(nk, 512)
                    S.op("act", lambda e: e.activation(out=PT_[:, 0:h0], in_=PS[2][:, 0:h0], func=AF.Copy),
                         R=[("PS", 2, 0)], W=[(ptk, 0)])
                    if nk > 512:
                        S.op("dve", lambda e: e.tensor_copy(out=PT_[:, 512:nk], in_=PS[2][:, 512:nk]),
                             R=[("PS", 2, 1)], W=[(ptk, 1)])

                def emit_G(u, ui):
                    slot = ui % 8
                    bank = (ui // 8) % 2
                    ak = ("PS", 3, bank)
                    ap_ = PS[3][:, bank * 512:(bank + 1) * 512]
                    PT_, ptk = PTB[ui % 2], ("PT", ui % 2)
                    chunks = [(u["vv"], u["vn"], b, jj) for jj, b in enumerate(u["vblocks"])]
                    if u["ctx"]:
                        chunks += [(VC, None, b, 4 + b) for b in range(PAST // 128)]
                    for ci, (vv, vn, b, jj) in enumerate(chunks):
                        rk = [(vn, sx, b)] if vn is not None else [VCk]
                        for (lo, hi) in ((0, 64), (64, 128)):
                            S.op("pe", lambda e, vv=vv, b=b, jj=jj, lo=lo, hi=hi, ci=ci: e.matmul(
                                ap_[lo:hi, slot * 64:(slot + 1) * 64], lhsT=vv[:, b, lo:hi], rhs=PT_[:, jj * 128 + lo:jj * 128 + hi],
                                start=(ci == 0), stop=(ci == len(chunks) - 1), tile_position=((0, lo) if lo else None)),
                                R=rk + [(ptk, jj // 4)], W=[ak])
                    if slot == 7 or ui == nu - 1:
                        qs = units[ui - slot]["q0"]
                        nq = (slot + 1) * 64
                        S.op("dve", lambda e, qs=qs, nq=nq, ap_=ap_: e.tensor_tensor(
                            out=GT[:, n, qs:qs + nq], in0=ap_[:, 0:nq], in1=SZ[:, qs:qs + nq], op=ALU.mult),
                            R=[ak] + [("SZ", sx, x) for x in g.tis(qs, nq)], W=[("GT", n, x) for x in g.tis(qs, nq)])

                units = units[:DBG.get('units', len(units))]
                nu = len(units)

                def prologue():
                    emit_A(units[0], 0)
                    if nu > 1:
                        emit_A(units[1], 1)
                    emit_B(units[0], 0)
                    emit_C(units[0], 0)
                    emit_D(units[0], 0)

                if is_s:
                    prologue()
                    state["pool"] = [(2, 0), (2, 1), (3, 0), (3, 1)]
                for ti in range(len(g.tiles)):
                    r_ = proj_mm(wb, wk, 384, g, ti)
                    yield "p"
                    evz(*r_)

                for (VV, off, vname) in ((V0, 0, "V0"),):
                    blocks = []
                    b = 0
                    while off + 128 * b < T:
                        blocks.append((b, off + 128 * b, min(128, T - off - 128 * b)))
                        b += 1
                    for q0 in range(0, len(blocks), 4):
                        quad = blocks[q0:q0 + 4]
                        ps, pk = projps()
                        for qi, (b, t0, m) in enumerate(quad):
                            for c in range(NCH):
                                S.op("pe", lambda e, c=c, qi=qi, t0=t0, m=m: e.matmul(
                                    ps[:m, qi * 128:(qi + 1) * 128], lhsT=HTG[g.gi][:, c, t0:t0 + m], rhs=wb[:, c, 256:384],
                                    start=(c == 0), stop=(c == NCH - 1)),
                                    R=[wk] + [("HT", g.gi, c, x) for x in g.tis(t0, m)], W=[pk])
                        yield "p"
                        for qi, (b, t0, m) in enumerate(quad):
                            S.op("dve", lambda e, qi=qi, b=b, m=m: e.tensor_copy(out=VV[:m, b, :], in_=ps[:m, qi * 128:(qi + 1) * 128]),
                                 R=[pk], W=[(vname, sx, b)])
                            if not is_s and 'vo' not in DBG.get('skip', ''):
                                kb = kos["k"] % 2
                                kos["k"] += 1
                                S.op("act", lambda e, qi=qi, m=m, kb=kb: e.activation(out=KO[kb][:m, 0:128], in_=ps[:m, qi * 128:(qi + 1) * 128], func=AF.Copy),
                                     R=[pk], W=[("KO", kb)])
                                S.dma("sp", dq["ko%d" % kb], nv_d[j, t0:t0 + m, n * 128:(n + 1) * 128], KO[kb][:m, 0:128], R=[("KO", kb)])

                if is_s:
                    S.dma("sp", dq["v1"], V1[0:64, 0:NB, :], V0[64:128, 0:NB, :],
                          R=[("V0", 0, b) for b in range(NB)], W=[("V1", 0, b) for b in range(NB)])
                    S.dma("sp", dq["v1"], V1[64:128, 0:NB - 1, :], V0[0:64, 1:NB, :],
                          R=[("V0", 0, b) for b in range(NB)], W=[("V1", 0, b) for b in range(NB)])
                if is_s:
                    state["pool"] = PP_ALL
                yield "ready"
                if DBG.get('stage', 9) < 2:
                    return
                if not is_s:
                    prologue()
                    yield "primed"
                for k in range(nu + 3):
                    if k + 1 < nu:
                        emit_B(units[k + 1], k + 1)
                    if 0 <= k - 2 < nu:
                        emit_F(units[k - 2], k - 2)
                    if k + 1 < nu:
                        emit_C(units[k + 1], k + 1)
                        emit_D(units[k + 1], k + 1)
                    if 0 <= k - 3 < nu:
                        emit_G(units[k - 3], k - 3)
                    if k + 2 < nu:
                        emit_A(units[k + 2], k + 2)
                    if is_s:
                        filler(k, nu)
                    if 0 <= k - 1 < nu:
                        emit_E(units[k - 1], k - 1)
                    if not is_s:
                        filler(k, nu)

            if NSET > 1:
                state["pool"] = [(0, 1), (1, 1), (2, 1)]
            projdone = [False] * npairs
            qkdone = [False] * npairs
            gens = []

            def advance(m, stop_at_qk=False):
                if m >= npairs or projdone[m] or (stop_at_qk and qkdone[m]):
                    return
                r = next(gens[m])
                if r == "qk":
                    qkdone[m] = True
                elif r == "ready":
                    projdone[m] = True

            primed = [False] * (npairs + 1)

            def make_filler(m):
                if NSET > 1:
                    def fp(k, nu):
                        if m >= npairs:
                            return
                        if not projdone[m]:
                            advance(m)
                        elif k >= nu - 1 and not primed[m]:
                            primed[m] = True
                            assert next(gens[m]) == "primed"
                    return fp

                def f(k, nu):
                    if k >= nu - 2:
                        advance(m, True)
                        advance(m, True)
                        advance(m, True)
                return f

            for n_ in range(npairs):
                gens.append(pair_body(n_, make_filler(n_ + 1)))
            for n_ in range(npairs):
                while not projdone[n_]:
                    advance(n_)
                for _ in gens[n_]:
                    pass
            state["pool"] = PP_ALL

            out_proj(g, i, "nwout", j, *hook())

        def final_norm_gen(g):
            Xg = X[g.gi]
            y_d = ysT_d if g.gi == 0 else ypT_d
            for ti, (t0, nt) in enumerate(g.tiles):
                t1, k1 = rms_rstd(g, t0, nt, lambda c: ("X", g.gi, c))
                for c in range(NCH):
                    t2, k2 = tmp()
                    S.op("dve", lambda e, c=c, t2=t2: e.scalar_tensor_tensor(
                        out=t2[:, :nt], in0=Xg[:, c, t0:t0 + nt], scalar=PV[:, 4, c:c + 1], in1=t1[:, :nt],
                        op0=ALU.mult, op1=ALU.mult), R=[("X", g.gi, c), "PVEC", k1], W=[k2])
                    S.dma("sp", tsem[k2[1]], y_d[c * 128:(c + 1) * 128, t0:t0 + nt], t2[:, :nt], R=[k2])
                    yield

        def final_norm(g):
            for _ in final_norm_gen(g):
                pass

        groups = [g_ for g_ in [GS, GP] if g_.name in DBG.get('groups', 'SP')]
        both = (len(groups) == 2)

        def run_layer(i, g, hook):
            ada_flush(i)
            with ExitStack() as lst:
                if i % 2 == 0:
                    conv_layer(i, g, lst, hook)
                else:
                    na_layer(i, g, lst, hook)
                S.barrier()

        if depth == 0:
            for g in groups:
                final_norm(g)
        elif not both:
            g = groups[0]
            for i in range(depth):
                modulate(g, i)
                run_layer(i, g, lambda: (None, 0))
            final_norm(g)
        else:
            modulate(GS, 0)
            for i in range(depth):
                run_layer(i, GS, lambda i=i: (modulate_gen(GP, i), 9 * len(GP.tiles)))

                def hook_p(i=i):
                    if i + 1 < depth:
                        ada_flush(i + 1)
                        return modulate_gen(GS, i + 1), 9 * len(GS.tiles)
                    return final_norm_gen(GS), 8 * len(GS.tiles)
                run_layer(i, GP, hook_p)
            final_norm(GP)
        S.barrier()
    return nc, WREC


_NAMES = ["x_prompt", "x_sample", "c", "cache_k", "cache_v", "c_ctx", "norm_g", "ada_w", "ada_b",
          "conv_w_in", "conv_dw_w", "conv_dw_b", "conv_ln_g", "conv_ln_b", "conv_w_out",
          "na_w_in", "na_rpb", "na_w_out", "final_g"]


def _fm(v):
    return np.ascontiguousarray(np.asarray(v, np.float32).reshape(NCH, 128).T)


def _bias_tables(na_rpb):
    qc = np.arange(GW)[:, None]
    kc = np.arange(GW)[None, :]
    qstart = np.clip(qc - 8, 0, GW - 16)
    valid = (kc >= qstart) & (kc < qstart + 16)
    dc = np.clip(kc - qc + 15, 0, 30)
    tm = np.empty((2, 8, 128, 15, GW), np.float32)
    for j in range(2):
        for h in range(16):
            g = na_rpb[j, h][:, dc]
            g = np.where(valid[None], g, np.float32(NEG)).transpose(1, 0, 2)
            tm[j, h // 2, (h % 2) * 64:(h % 2) * 64 + 64] = g
    return np.ascontiguousarray(tm.reshape(2, 8, 128, 15 * GW))


def make_in_maps(inp):
    a = {k: np.asarray(inp[k], dtype=np.float32) for k in _NAMES}
    pv_list = [a["norm_g"][i] for i in range(4)] + [a["final_g"]] + \
              [a["conv_dw_b"][j] for j in range(2)] + [a["conv_ln_g"][j] for j in range(2)] + \
              [a["conv_ln_b"][j] for j in range(2)]
    pv = np.ascontiguousarray(np.stack([_fm(v) for v in pv_list], axis=1))
    adab = np.ascontiguousarray(np.stack([a["ada_b"][i].reshape(24, 128).T for i in range(4)], axis=1))
    dww = np.ascontiguousarray(np.stack(
        [a["conv_dw_w"][j].T.reshape(NCH, 128, CK).transpose(1, 0, 2) for j in range(2)], axis=1))
    tm = _bias_tables(a["na_rpb"])
    ident = np.eye(128, dtype=np.float32)
    cwi = a["conv_w_in"]
    cab = np.ascontiguousarray(np.stack(
        [np.concatenate([cwi[j][:, n * 128:(n + 1) * 128], cwi[j][:, D + n * 128:D + (n + 1) * 128]], axis=1)
         for j in range(2) for n in range(8)], axis=0))
    cz = np.ascontiguousarray(cwi[:, :, 2 * D:3 * D])
    nwi = a["na_w_in"]
    nwp = np.ascontiguousarray(np.stack(
        [np.concatenate([nwi[j][:, part * D + n * 128:part * D + (n + 1) * 128] for part in range(4)], axis=1)
         for j in range(2) for n in range(8)], axis=0))
    dwwr = np.ascontiguousarray(np.roll(dww, -64, axis=0))
    shared = dict(pv=pv, adab=adab, dww=dww, dwwr=dwwr, tm=tm, ident=ident, ada_w=a["ada_w"], cab=cab, cz=cz,
                  conv_w_out=a["conv_w_out"], nwp=nwp, na_w_out=a["na_w_out"])
    maps = []
    for c in range(NCORES):
        b = c // 2
        g0 = 0 if c % 2 == 0 else (32 - LROWS)
        m = dict(shared)
        m["xsT"] = np.ascontiguousarray(a["x_sample"][b, g0 * GW:g0 * GW + TS].T)
        m["xpT"] = np.ascontiguousarray(a["x_prompt"][2 * c:2 * c + 2].reshape(TP, D).T)
        m["cvec"] = np.ascontiguousarray(np.stack([_fm(a["c"][b]), _fm(a["c_ctx"])], axis=2))
        m["ckT"] = np.ascontiguousarray(a["cache_k"][b].reshape(2, PAST, D).transpose(0, 2, 1))
        m["cv"] = np.ascontiguousarray(a["cache_v"][b].reshape(2, PAST, D))
        maps.append(m)
    return maps


def assemble(results):
    y_prompt = np.empty((16, SEQ, D), np.float32)
    y_sample = np.empty((4, 2048, D), np.float32)
    nk = np.empty((16, 2, SEQ, 16, 64), np.float32)
    nvv = np.empty((16, 2, SEQ, 16, 64), np.float32)
    for c in range(NCORES):
        r = results[c]
        b = c // 2
        ys = r["ysT"].T
        if c % 2 == 0:
            y_sample[b, 0:1024] = ys[0:1024]
        else:
            lo = (16 - (32 - LROWS)) * GW
            y_sample[b, 1024:2048] = ys[lo:lo + 1024]
        y_prompt[2 * c:2 * c + 2] = r["ypT"].T.reshape(2, SEQ, D)
        for jl in range(2):
            kt = r["nkT"][jl].T.reshape(2, SEQ, 16, 64)
            vt = r["nv"][jl].reshape(2, SEQ, 16, 64)
            nk[2 * c:2 * c + 2, jl] = kt
            nvv[2 * c:2 * c + 2, jl] = vt
    return y_prompt, y_sample, nk, nvv


_CACHE = {}


def kernel(**inputs):
    if "nc" not in _CACHE:
        _, rec = build_program()
        _CACHE["nc"] = build_program(sched=rec)[0]
    nc = _CACHE["nc"]
    in_maps = make_in_maps(inputs)
    res = run_bass_kernel_spmd(nc, in_maps, core_ids=list(range(NCORES)))
    return assemble(res.results)
```

```python
import numpy as np
from contextlib import ExitStack
import concourse.bass as bass
import concourse.mybir as mybir
from concourse.bass_utils import run_bass_kernel_spmd

F32 = mybir.dt.float32
BF16 = mybir.dt.bfloat16
ALU = mybir.AluOpType
AF = mybir.ActivationFunctionType
AX = mybir.AxisListType

NCORES = 8
DEPTH = 4
D = 1024
NCH = 8
LROWS = 26
GW = 64
TS = LROWS * GW
TP = 512
SEQ = 256
CK = 31
PAST = 512
NEG = -30000.0
EPS = 1e-6
DBG = {}


class Sched:
    def __init__(self, nc, stack):
        self.nc = nc
        self.stack = stack
        self.eng = {"pe": nc.tensor, "act": nc.scalar, "dve": nc.vector, "pool": nc.gpsimd, "sp": nc.sync}
        self.sem = {k: stack.enter_context(nc.semaphore("s_" + k)) for k in self.eng}
        self.cnt = {k: 0 for k in self.eng}
        self.waited = {k: {} for k in self.eng}
        self.last_w = {}
        self.readers = {}
        self.dma_cnt = {}
        self.dma_keys = []

    def _wait(self, e, semkey, val):
        kid = semkey if isinstance(semkey, str) else semkey[2]
        if e == "pe" and semkey == "pe":
            return
        if self.waited[e].get(kid, 0) >= val:
            return
        self.waited[e][kid] = val
        sem = self.sem[semkey] if isinstance(semkey, str) else semkey[1]
        self.eng[e].wait_ge(sem, val)

    def _deps(self, e, R, W, relaxed=False):
        for r in R:
            lw = self.last_w.get(r)
            if lw is not None:
                self._wait(e, lw[0], lw[1])
            if isinstance(r, tuple) and r[0] == "PS":
                for rd in self.readers.get(r, ()):
                    if rd[0] != e:
                        self._wait(e, rd[0], rd[1])
        for w in W:
            lw = self.last_w.get(w)
            if lw is not None and not (relaxed and lw[0] == e):
                self._wait(e, lw[0], lw[1])
            for rd in self.readers.get(w, ()):
                if not (relaxed and rd[0] == e):
                    self._wait(e, rd[0], rd[1])

    def _commit(self, tok, R, W):
        for r in R:
            self.readers.setdefault(r, []).append(tok)
        for w in W:
            self.last_w[w] = tok
            self.readers[w] = []

    def op(self, e, fn, R=(), W=()):
        self._deps(e, R, W)
        ins = fn(self.eng[e])
        self.cnt[e] += 1
        ins.then_inc(self.sem[e], 1)
        self._commit((e, self.cnt[e]), R, W)
        return ins

    def new_dma_sem(self, name):
        s = self.stack.enter_context(self.nc.semaphore(name))
        key = ("dma", s, name)
        self.dma_cnt[name] = 0
        self.dma_keys.append(key)
        return key

    def dma(self, q, semkey, out, in_, R=(), W=()):
        self._deps(q, R, W)
        ins = self.eng[q].dma_start(out=out, in_=in_)
        self.dma_cnt[semkey[2]] += 16
        ins.then_inc(semkey[1], 16)
        self._commit((semkey, self.dma_cnt[semkey[2]]), R, W)
        return ins

    def barrier(self):
        for e in self.eng:
            for k in self.eng:
                if k != e and self.cnt[k]:
                    self._wait(e, k, self.cnt[k])
            for key in self.dma_keys:
                if self.dma_cnt[key[2]]:
                    self._wait(e, key, self.dma_cnt[key[2]])
        self.last_w = {}
        self.readers = {}


class Grp:
    def __init__(self, name, gi, T, v, seqs, tile):
        self.name = name
        self.gi = gi
        self.T = T
        self.v = v
        self.seqs = seqs
        self.tiles = []
        for (s0, sl) in seqs:
            t = s0
            while t < s0 + sl:
                n = min(tile, s0 + sl - t)
                self.tiles.append((t, n))
                t += n

    def tidx(self, t):
        for i, (t0, n) in enumerate(self.tiles):
            if t0 <= t < t0 + n:
                return i
        raise ValueError(t)

    def tis(self, t0, n):
        return sorted({self.tidx(t0), self.tidx(t0 + n - 1)})

    def uoff(self, t):
        for si, (s0, sl) in enumerate(self.seqs):
            if s0 <= t < s0 + sl:
                return 15 + t + 15 * si
        raise ValueError(t)

    def ulen(self):
        return self.T + 15 * (len(self.seqs) + 1)


def build_program(depth=DEPTH, sched=None):
    nc = bass.Bass("TRN2", target_bir_lowering=False)
    WREC = []

    def din(name, shape):
        return nc.dram_tensor(name, list(shape), F32, kind="ExternalInput").ap()

    def dout(name, shape):
        return nc.dram_tensor(name, list(shape), F32, kind="ExternalOutput").ap()

    xsT_d = din("xsT", [D, TS])
    xpT_d = din("xpT", [D, TP])
    cvec_d = din("cvec", [128, NCH, 2])
    pv_d = din("pv", [128, 11, NCH])
    adab_d = din("adab", [128, 4, 24])
    dww_d = din("dww", [128, 2, NCH, CK])
    dwwr_d = din("dwwr", [128, 2, NCH, CK])
    ckT_d = din("ckT", [2, D, PAST])
    cv_d = din("cv", [2, PAST, D])
    tm_d = din("tm", [2, 8, 128, 960])
    ident_d = din("ident", [128, 128])
    ada_w_d = din("ada_w", [4, D, 3 * D])
    cab_d = din("cab", [16, D, 256])
    cz_d = din("cz", [2, D, D])
    cwout_d = din("conv_w_out", [2, D, D])
    nwp_d = din("nwp", [16, D, 512])
    nwout_d = din("na_w_out", [2, D, D])

    ysT_d = dout("ysT", [D, TS])
    ypT_d = dout("ypT", [D, TP])
    nkT_d = dout("nkT", [2, D, TP])
    nv_d = dout("nv", [2, TP, D])

    GS = Grp("S", 0, TS, 0, [(0, TS)], 512)
    GP = Grp("P", 1, TP, 1, [(0, SEQ), (SEQ, SEQ)], 256)

    with ExitStack() as st:
        S = Sched(nc, st)

        def sb(name, shape, dt, stack=st):
            return stack.enter_context(nc.sbuf_tensor(name, list(shape), dt))

        XS = sb("XS", [128, NCH, TS], F32)
        XP = sb("XP", [128, NCH, TP], F32)
        X = {0: XS, 1: XP}
        HT = sb("HT", [128, NCH, TS], BF16)
        GT = sb("GT", [128, NCH, TS], BF16)
        NWB = 3
        WB = [sb("WB%d" % k, [128, NCH, 512], BF16) for k in range(NWB)]
        SQR = [sb("SQ%d" % k, [128, 512], BF16) for k in range(2)]
        HTP = sb("HTP", [128, NCH, TP], BF16)
        HTG = {0: HT, 1: HTP}
        NT = 5
        TT = [sb("T%d" % k, [128, 512], F32) for k in range(NT)]
        TR = [sb("TR%d" % k, [128, 512], F32) for k in range(2)]
        IDB = sb("IDB", [128, 128], BF16)
        ONES = sb("ONES", [128, 128], BF16)
        EPSB = sb("EPSB", [128, 1], F32)
        NEG1 = sb("NEG1", [128, 1], F32)
        CV = sb("CV", [128, NCH, 2], F32)
        SCT = sb("SCT", [128, NCH, 2], BF16)
        PV = sb("PVEC", [128, 11, NCH], F32)
        ADAB = sb("ADAB", [128, 4, 24], F32)
        DWW = sb("DWW", [128, 2, NCH, CK], BF16)
        DWWR = sb("DWWR", [128, 2, NCH, CK], BF16)
        ID64 = sb("ID64", [128, 64], BF16)
        MODS = sb("MODS", [128, 4, 2, 24], F32)
        AMOD = sb("AMOD", [128, 4, 2, NCH], F32)
        GATE = sb("GATE", [128, 4, 2, NCH], F32)
        MSTG = sb("MSTG", [128, 48], F32)

        PS = [st.enter_context(nc.psum_tensor("PS%d" % k, [128, 1024], F32)) for k in range(4)]

        dq = {k: S.new_dma_sem("dq_" + k) for k in ("ld", "ld2", "w0", "w1", "w2", "aux", "idb", "v1", "u1", "u1b", "kc0", "kc1", "vc0", "vc1", "tm0", "tm1", "ko0", "ko1")}
        wsem = [dq["w0"], dq["w1"], dq["w2"]]
        tsem = [S.new_dma_sem("dq_t%d" % k) for k in range(NT)]

        state = {"t": 0, "wb": 0, "pp": 0, "sp": 0, "tr": 0}

        def tmp():
            k = state["t"] % NT
            state["t"] += 1
            return TT[k], ("T", k)

        def tmpr():
            k = state["tr"] % 2
            state["tr"] += 1
            return TR[k], ("TR", k)

        DR = {"ada_w": ada_w_d, "cab": cab_d, "cz": cz_d, "cwout": cwout_d, "nwp": nwp_d, "nwout": nwout_d}
        wst = {"idx": 0, "issued": 0}

        def w_issue(idx, spec):
            k = idx % NWB
            for (dlo, width, wname, li, slo) in spec:
                S.dma("pool", wsem[k], WB[k][:, :, dlo:dlo + width], wview(DR[wname][li])[:, :, slo:slo + width], W=[("WB", k)])

        def wnext(spec):
            idx = wst["idx"]
            wst["idx"] += 1
            if sched is None:
                WREC.append(spec)
                w_issue(idx, spec)
            else:
                assert sched[idx] == spec, (idx, sched[idx], spec)
                while wst["issued"] <= min(idx + 2, len(sched) - 1):
                    w_issue(wst["issued"], sched[wst["issued"]])
                    wst["issued"] += 1
            return WB[idx % NWB], ("WB", idx % NWB)

        PP_ALL = [(0, 0), (0, 1), (1, 0), (1, 1)]
        state["pool"] = PP_ALL

        def projps():
            pool_ = state["pool"]
            t, h = pool_[state["pp"] % len(pool_)]
            state["pp"] += 1
            return PS[t][:, h * 512:h * 512 + 512], ("PS", t, h)

        SP_ALL = [(2, 0), (2, 1)]
        state["spool"] = SP_ALL

        def statps():
            pool_ = state["spool"]
            t, h = pool_[state["sp"] % len(pool_)]
            state["sp"] += 1
            return PS[t][:, h * 512:h * 512 + 512], ("PS", t, h)

        def wview(w2d):
            return w2d.rearrange("(c p) n -> p c n", p=128)

        S.dma("sp", dq["aux"], CV[:], cvec_d, W=["CV"])
        S.dma("sp", dq["aux"], PV[:], pv_d, W=["PVEC"])
        S.dma("sp", dq["aux"], ADAB[:], adab_d, W=["ADAB"])
        S.dma("pool", dq["idb"], DWW[:], dww_d, W=["DWW"])
        S.dma("pool", dq["idb"], DWWR[:], dwwr_d, W=["DWWR"])
        S.dma("pool", dq["idb"], IDB[:], ident_d, W=["IDB"])
        S.op("dve", lambda e: e.memset(ONES[:], 1.0 / D), W=["ONES"])
        S.op("dve", lambda e: e.memset(EPSB[:], EPS), W=["EPSB"])
        S.op("dve", lambda e: e.memset(NEG1[:], -1.0), W=["NEG1"])
        S.barrier()
        S.dma("sp", dq["ld"], XS[:], xsT_d.rearrange("(c p) t -> p c t", p=128),
              W=[("X", 0, c) for c in range(NCH)])
        S.dma("sp", dq["ld2"], XP[:], xpT_d.rearrange("(c p) t -> p c t", p=128),
              W=[("X", 1, c) for c in range(NCH)])
        S.op("act", lambda e: e.activation(out=SCT[:], in_=CV[:], func=AF.Silu), R=["CV"], W=["SCT"])
        S.op("dve", lambda e: e.tensor_tensor(out=ID64[:], in0=IDB[:, 0:64], in1=IDB[:, 64:128], op=ALU.add), R=["IDB"], W=["ID64"])

        ada_q = [(i, blk) for i in range(depth) for blk in range(6)]

        def ada_step():
            if not ada_q:
                return
            i, blk = ada_q.pop(0)
            wb, wk = wnext([(0, 512, "ada_w", i, blk * 512)])
            aps = PS[3][:, 0:8]
            for jn in range(4):
                for c in range(NCH):
                    S.op("pe", lambda e, c=c, jn=jn, wb=wb: e.matmul(
                        aps[:, 2 * jn:2 * jn + 2], lhsT=wb[:, c, jn * 128:(jn + 1) * 128], rhs=SCT[:, c, :],
                        start=(c == 0), stop=(c == NCH - 1)),
                        R=[wk, "SCT"], W=[("PS", 3, 0)])
            S.op("dve", lambda e: e.tensor_copy(out=MSTG[:, 8 * blk:8 * blk + 8], in_=aps), R=[("PS", 3, 0)], W=["MSTG"])
            if blk == 5:
                m3 = MSTG[:].rearrange("p (n v) -> p n v", v=2)
                for v in range(2):
                    S.op("dve", lambda e, v=v: e.tensor_tensor(out=MODS[:, i, v, :], in0=m3[:, :, v], in1=ADAB[:, i, :], op=ALU.add),
                         R=["MSTG", "ADAB"], W=[("MODS", i)])
                    S.op("dve", lambda e, v=v: e.scalar_tensor_tensor(
                        out=AMOD[:, i, v, :], in0=MODS[:, i, v, 8:16], scalar=1.0, in1=PV[:, i, :],
                        op0=ALU.add, op1=ALU.mult), R=[("MODS", i), "PVEC"], W=[("AMOD", i)])
                    S.op("dve", lambda e, v=v: e.tensor_scalar(out=GATE[:, i, v, :], in0=MODS[:, i, v, 16:24],
                                                               scalar1=(0.5 if i % 2 == 1 else 1.0), scalar2=None, op0=ALU.mult),
                         R=[("MODS", i)], W=[("GATE", i)])

        def ada_flush(upto):
            while ada_q and ada_q[0][0] <= upto:
                ada_step()

        ada_flush(0)

        def rms_rstd(g, t0, nt, xr):
            Xg = X[g.gi]
            ps, pk = statps()
            for c in range(NCH):
                sq = SQR[c % 2]
                S.op("act", lambda e, c=c, sq=sq: e.activation(out=sq[:, :nt], in_=Xg[:, c, t0:t0 + nt], func=AF.Square),
                     R=[xr(c)], W=[("SQ", c % 2)])
                S.op("pe", lambda e, c=c, sq=sq: e.matmul(ps[:, :nt], lhsT=ONES[:], rhs=sq[:, :nt],
                                                          start=(c == 0), stop=(c == NCH - 1)),
                     R=["ONES", ("SQ", c % 2)], W=[pk])
            t1, k1 = tmpr()
            S.op("act", lambda e: e.activation(out=t1[:, :nt], in_=ps[:, :nt], func=AF.Sqrt, bias=EPSB[:, 0:1], scale=1.0),
                 R=[pk, "EPSB"], W=[k1])
            S.op("dve", lambda e: e.reciprocal(out=t1[:, :nt], in_=t1[:, :nt]), R=[k1], W=[k1])
            return t1, k1

        def modulate_gen(g, i):
            Xg = X[g.gi]
            for ti, (t0, nt) in enumerate(g.tiles):
                t1, k1 = rms_rstd(g, t0, nt, lambda c: ("X", g.gi, c))
                yield
                for c in range(NCH):
                    t2, k2 = tmp()
                    S.op("dve", lambda e, c=c, t2=t2: e.scalar_tensor_tensor(
                        out=t2[:, :nt], in0=Xg[:, c, t0:t0 + nt], scalar=AMOD[:, i, g.v, c:c + 1], in1=t1[:, :nt],
                        op0=ALU.mult, op1=ALU.mult), R=[("X", g.gi, c), ("AMOD", i), k1], W=[k2])
                    S.op("act", lambda e, c=c, t2=t2: e.activation(
                        out=HTG[g.gi][:, c, t0:t0 + nt], in_=t2[:, :nt], func=AF.Identity,
                        bias=MODS[:, i, g.v, c:c + 1], scale=1.0), R=[k2, ("MODS", i)], W=[("HT", g.gi, c, ti)])
                    yield

        def modulate(g, i):
            for _ in modulate_gen(g, i):
                pass

        def proj_mm(wb, wk, col0, g, ti):
            t0, nt = g.tiles[ti]
            ps, pk = projps()
            for c in range(NCH):
                S.op("pe", lambda e, c=c: e.matmul(ps[:, :nt], lhsT=wb[:, c, col0:col0 + 128], rhs=HTG[g.gi][:, c, t0:t0 + nt],
                                                    start=(c == 0), stop=(c == NCH - 1)),
                     R=[wk, ("HT", g.gi, c, ti)], W=[pk])
            return ps, pk, t0, nt, ti

        def proj_fm(wb, wk, col0, g, ti, evac):
            evac(*proj_mm(wb, wk, col0, g, ti))

        def out_proj(g, i, wname, j, bg=None, bg_steps=0):
            Xg = X[g.gi]
            ngroups = NCH * len(g.tiles)
            per = -(-bg_steps // ngroups) if bg is not None else 0
            for blk in range(2):
                wb, wk = wnext([(0, 512, wname, j, blk * 512)])
                for jm in range(4):
                    m = blk * 4 + jm
                    for ti, (t0, nt) in enumerate(g.tiles):
                        ps, pk = projps()
                        for n in range(NCH):
                            S.op("pe", lambda e, n=n: e.matmul(ps[:, :nt], lhsT=wb[:, n, jm * 128:(jm + 1) * 128],
                                                                rhs=GT[:, n, t0:t0 + nt], start=(n == 0), stop=(n == NCH - 1)),
                                 R=[wk, ("GT", n, ti)], W=[pk])
                        S.op("dve", lambda e: e.scalar_tensor_tensor(
                            out=Xg[:, m, t0:t0 + nt], in0=ps[:, :nt], scalar=GATE[:, i, g.v, m:m + 1],
                            in1=Xg[:, m, t0:t0 + nt], op0=ALU.mult, op1=ALU.add),
                            R=[pk, ("GATE", i), ("X", g.gi, m)], W=[("X", g.gi, m)])
                        for _ in range(per):
                            if bg is not None and next(bg, "end") == "end":
                                bg = None
            if bg is not None:
                for _ in bg:
                    pass

        def conv_layer(i, g, lst, hook):
            j = i // 2
            UL = g.ulen()
            U = [sb("U%d_%d_%d" % (i, g.gi, k), [128, UL], BF16, lst) for k in range(2)]
            NPB = 2 if g.gi == 1 else 1
            U1s = [sb("U1_%d_%d_%d" % (i, g.gi, k), [128, UL], BF16, lst) for k in range(NPB)]
            DGAs = [sb("DGA%d_%d_%d" % (i, g.gi, k), [128, CK, 64], BF16, lst) for k in range(NPB)]
            DGBs = [sb("DGB%d_%d_%d" % (i, g.gi, k), [128, CK, 64], BF16, lst) for k in range(NPB)]
            NS0 = (CK + 1) // 2
            MR = sb("MR%d_%d" % (i, g.gi), [128, 2, g.T], F32, lst)
            if DBG.get('mem'): print('conv mem remaining', g.name, nc.sbuf_bytes_remaining)
            for k in range(2):
                S.op("pool", lambda e, k=k: e.memset(U[k][:], 0.0), W=[("U", k)])
            def glu(n):
                wb, wk = wnext([(0, 256, "cab", j * 8 + n, 0)])
                ub = n % 2
                for ti, (t0, nt) in enumerate(g.tiles):
                    pa, ka = projps()
                    pb, kb = projps()
                    for (ps, pk, col0) in ((pa, ka, 0), (pb, kb, 128)):
                        for c in range(NCH):
                            S.op("pe", lambda e, c=c, ps=ps, col0=col0: e.matmul(
                                ps[:, :nt], lhsT=wb[:, c, col0:col0 + 128], rhs=HTG[g.gi][:, c, t0:t0 + nt],
                                start=(c == 0), stop=(c == NCH - 1)), R=[wk, ("HT", g.gi, c, ti)], W=[pk])
                    t1, k1 = tmp()
                    S.op("act", lambda e: e.activation(out=t1[:, :nt], in_=pb[:, :nt], func=AF.Sigmoid), R=[kb], W=[k1])
                    uo = g.uoff(t0)
                    S.op("dve", lambda e: e.tensor_tensor(out=U[ub][:, uo:uo + nt], in0=pa[:, :nt], in1=t1[:, :nt], op=ALU.mult),
                         R=[ka, k1], W=[("U", ub)])

            def prep(n):
                ub = n % 2
                pb_ = n % NPB
                U1, DGA, DGB = U1s[pb_], DGAs[pb_], DGBs[pb_]
                S.op("dve", lambda e: e.tensor_tensor(
                    out=DGA[:], in0=ID64[:].unsqueeze(1).to_broadcast([128, CK, 64]),
                    in1=DWW[:, j, n, :].unsqueeze(2).to_broadcast([128, CK, 64]), op=ALU.mult),
                    R=["ID64", "DWW"], W=[("DGA", pb_)])
                S.op("dve", lambda e: e.tensor_tensor(
                    out=DGB[:], in0=ID64[:].unsqueeze(1).to_broadcast([128, CK, 64]),
                    in1=DWWR[:, j, n, :].unsqueeze(2).to_broadcast([128, CK, 64]), op=ALU.mult),
                    R=["ID64", "DWWR"], W=[("DGB", pb_)])
                S.dma("sp", dq["u1b" if pb_ else "u1"], U1[0:64, :], U[ub][64:128, :], R=[("U", ub)], W=[("U1", pb_)])
                S.dma("sp", dq["u1b" if pb_ else "u1"], U1[64:128, :], U[ub][0:64, :], R=[("U", ub)], W=[("U1", pb_)])

            def conv(n):
                ub = n % 2
                for ti, (t0, nt) in enumerate(g.tiles):
                    pt_ = 2 + (ti % 2)
                    banks = [(PS[pt_][:, 0:512], ("PS", pt_, 0)), (PS[pt_][:, 512:1024], ("PS", pt_, 1))]
                    uo = g.uoff(t0)
                    for sl in range(NS0):
                        for (ih, jh) in ((0, 0), (1, 1), (1, 0), (0, 1)):
                            own = (ih == jh)
                            k = sl if own else NS0 + sl
                            if k >= CK:
                                continue
                            first = (sl == 0)
                            last = (k == NS0 - 1) if own else (k == CK - 1)
                            src, skey = (U[ub], ("U", ub)) if own else (U1s[n % NPB], ("U1", n % NPB))
                            dg, dkey = (DGAs[n % NPB], ("DGA", n % NPB)) if own else (DGBs[n % NPB], ("DGB", n % NPB))
                            bk, bkey = banks[ih]
                            S.op("pe", lambda e, ih=ih, jh=jh, k=k, src=src, dg=dg, bk=bk, first=first, last=last: e.matmul(
                                bk[64 * jh:64 * jh + 64, :nt], lhsT=dg[64 * ih:64 * ih + 64, k, :],
                                rhs=src[64 * ih:64 * ih + 64, uo + k - 15:uo + k - 15 + nt],
                                start=first, stop=last, tile_position=((64 * ih, 64 * jh) if (ih or jh) else None)),
                                R=[dkey, skey], W=[bkey])
                    t1, k1 = tmp()
                    t2, k2 = tmp()
                    S.op("act", lambda e: e.activation(out=t1[:, :nt], in_=banks[0][0][:, :nt], func=AF.Identity,
                                                       bias=PV[:, 5 + j, n:n + 1], scale=1.0),
                         R=[banks[0][1], "PVEC"], W=[k1])
                    S.op("dve", lambda e: e.tensor_copy(out=t2[:, :nt], in_=banks[1][0][:, :nt]), R=[banks[1][1]], W=[k2])
                    S.op("pool", lambda e: e.tensor_tensor(out=GT[:, n, t0:t0 + nt], in0=t1[:, :nt], in1=t2[:, :nt], op=ALU.add),
                         R=[k1, k2], W=[("GT", n, ti)])

            bg0, bg_total = hook()
            bgs = [bg0]
            bg_per = -(-bg_total * 2 // (3 * NCH)) if bg0 is not None else 0
            glu(0)
            prep(0)
            for n in range(NCH):
                if n + 1 < NCH:
                    glu(n + 1)
                    if NPB > 1:
                        prep(n + 1)
                conv(n)
                if n + 1 < NCH and NPB == 1:
                    prep(n + 1)
                ada_step()
                for _ in range(bg_per):
                    if bgs[0] is not None and next(bgs[0], "end") == "end":
                        bgs[0] = None

            state["spool"] = SP_ALL
            for ti, (t0, nt) in enumerate(g.tiles):
                pm, km = statps()
                pq, kq = statps()
                for c in range(NCH):
                    S.op("pe", lambda e, c=c: e.matmul(pm[:, :nt], lhsT=ONES[:], rhs=GT[:, c, t0:t0 + nt],
                                                        start=(c == 0), stop=(c == NCH - 1)), R=["ONES", ("GT", c, ti)], W=[km])
                for c in range(NCH):
                    sq = SQR[c % 2]
                    S.op("act", lambda e, c=c, sq=sq: e.activation(out=sq[:, :nt], in_=GT[:, c, t0:t0 + nt], func=AF.Square),
                         R=[("GT", c, ti)], W=[("SQ", c % 2)])
                    S.op("pe", lambda e, c=c, sq=sq: e.matmul(pq[:, :nt], lhsT=ONES[:], rhs=sq[:, :nt],
                                                              start=(c == 0), stop=(c == NCH - 1)), R=["ONES", ("SQ", c % 2)], W=[kq])
                S.op("act", lambda e: e.activation(out=MR[:, 0, t0:t0 + nt], in_=pm[:, :nt], func=AF.Copy), R=[km], W=[("MR", 0, ti)])
                t1, k1 = tmp()
                S.op("dve", lambda e: e.tensor_tensor(out=t1[:, :nt], in0=MR[:, 0, t0:t0 + nt], in1=MR[:, 0, t0:t0 + nt], op=ALU.mult),
                     R=[("MR", 0, ti)], W=[k1])
                S.op("dve", lambda e: e.tensor_tensor(out=t1[:, :nt], in0=pq[:, :nt], in1=t1[:, :nt], op=ALU.subtract),
                     R=[kq, k1], W=[k1])
                S.op("act", lambda e: e.activation(out=MR[:, 1, t0:t0 + nt], in_=t1[:, :nt], func=AF.Sqrt, bias=EPSB[:, 0:1], scale=1.0),
                     R=[k1, "EPSB"], W=[("MR", 1, ti)])
                S.op("dve", lambda e: e.reciprocal(out=MR[:, 1, t0:t0 + nt], in_=MR[:, 1, t0:t0 + nt]),
                     R=[("MR", 1, ti)], W=[("MR", 1, ti)])

            for blk in range(2):
                wb, wk = wnext([(0, 512, "cz", j, blk * 512)])
                for jn in range(4):
                    n = blk * 4 + jn
                    for ti in range(len(g.tiles)):
                        def evac(ps, pk, t0, nt, ti, n=n):
                            t1, k1 = tmp()
                            S.op("act", lambda e: e.activation(out=t1[:, :nt], in_=ps[:, :nt], func=AF.Silu), R=[pk], W=[k1])
                            t2, k2 = tmp()
                            S.op("dve", lambda e: e.tensor_tensor(out=t2[:, :nt], in0=GT[:, n, t0:t0 + nt], in1=MR[:, 0, t0:t0 + nt], op=ALU.subtract),
                                 R=[("GT", n, ti), ("MR", 0, ti)], W=[k2])
                            S.op("dve", lambda e: e.tensor_tensor(out=t2[:, :nt], in0=t2[:, :nt], in1=MR[:, 1, t0:t0 + nt], op=ALU.mult),
                                 R=[k2, ("MR", 1, ti)], W=[k2])
                            t3, k3 = tmp()
                            S.op("act", lambda e: e.activation(out=t3[:, :nt], in_=t2[:, :nt], func=AF.Silu,
                                                               bias=PV[:, 9 + j, n:n + 1], scale=PV[:, 7 + j, n:n + 1]),
                                 R=[k2, "PVEC"], W=[k3])
                            S.op("dve", lambda e: e.tensor_tensor(out=GT[:, n, t0:t0 + nt], in0=t3[:, :nt], in1=t1[:, :nt], op=ALU.mult),
                                 R=[k3, k1], W=[("GT", n, ti)])
                        proj_fm(wb, wk, jn * 128, g, ti, evac)

            out_proj(g, i, "cwout", j, bgs[0], bg_total // 3)

        def na_layer(i, g, lst, hook):
            j = i // 2
            is_s = (g.gi == 0)
            T = g.T
            NB = (T + 127) // 128
            NSET = 1 if is_s else 2
            QTs = [sb("QT%d_%d_%d" % (i, g.gi, k), [128, T], BF16, lst) for k in range(NSET)]
            KTs = [sb("KT%d_%d_%d" % (i, g.gi, k), [128, T], BF16, lst) for k in range(NSET)]
            SZs = [sb("SZ%d_%d_%d" % (i, g.gi, k), [128, T], BF16, lst) for k in range(NSET)]
            V0s = [sb("V0%d_%d_%d" % (i, g.gi, k), [128, NB, 128], BF16, lst) for k in range(NSET)]
            V1 = sb("V1%d_%d" % (i, g.gi), [128, NB, 128], BF16, lst) if is_s else None
            if is_s:
                KC2 = [sb("KC%d_%d" % (i, k), [128, PAST], BF16, lst) for k in range(2)]
                VC2 = [sb("VC%d_%d" % (i, k), [128, PAST // 128, 128], BF16, lst) for k in range(2)]
                TM2 = [sb("TM%d_%d" % (i, k), [128, 960], BF16, lst) for k in range(2)]

                def load_ctx(n):
                    k = n % 2
                    S.dma("pool", dq["kc%d" % k], KC2[k][:], ckT_d[j, n * 128:(n + 1) * 128, :], W=[("KC", k)])
                    S.dma("pool", dq["vc%d" % k], VC2[k][:], cv_d[j].rearrange("(b p) f -> p b f", p=128)[:, :, n * 128:(n + 1) * 128], W=[("VC", k)])
                    S.dma("pool", dq["tm%d" % k], TM2[k][:], tm_d[j, n], W=[("TM", k)])
            PB_ = [sb("P%d_%d_%d" % (i, g.gi, k), [128, 1024], BF16, lst) for k in range(3)]
            PTB = [sb("PT%d_%d_%d" % (i, g.gi, k), [128, 1024], BF16, lst) for k in range(2)]
            DD = [sb("DD%d_%d_%d" % (i, g.gi, k), [128, 128], BF16, lst) for k in range(3)]
            STT = [sb("ST%d_%d_%d" % (i, g.gi, k), [128, 4], F32, lst) for k in range(2)]
            KO = [sb("KO%d_%d_%d" % (i, g.gi, k), [128, 512], F32, lst) for k in range(2)] if not is_s else None

            if DBG.get('mem'): print('NA mem remaining', g.name, nc.sbuf_bytes_remaining)
            kos = {"k": 0}

            npairs = DBG.get('pairs', NCH)

            def pair_body(n, filler):
                sx = n % NSET
                QT, KT, SZ, V0 = QTs[sx], KTs[sx], SZs[sx], V0s[sx]
                wb, wk = wnext([(0, 512, "nwp", j * 8 + n, 0)])
                if is_s:
                    if n == 0:
                        load_ctx(0)
                    KC, VC, TM = KC2[n % 2], VC2[n % 2], TM2[n % 2]
                    KCk, VCk, TMk = ("KC", n % 2), ("VC", n % 2), ("TM", n % 2)

                def evq(ps, pk, t0, nt, ti):
                    S.op("act", lambda e: e.activation(out=QT[:, t0:t0 + nt], in_=ps[:, :nt], func=AF.Identity, scale=0.125),
                         R=[pk], W=[("QT", sx, ti)])

                def evk(ps, pk, t0, nt, ti):
                    S.op("dve", lambda e: e.tensor_copy(out=KT[:, t0:t0 + nt], in_=ps[:, :nt]), R=[pk], W=[("KT", sx, ti)])
                    if not is_s and 'ko' not in DBG.get('skip', ''):
                        kb = kos["k"] % 2
                        kos["k"] += 1
                        S.op("act", lambda e: e.activation(out=KO[kb][:, :nt], in_=ps[:, :nt], func=AF.Copy), R=[pk], W=[("KO", kb)])
                        S.dma("sp", dq["ko%d" % kb], nkT_d[j, n * 128:(n + 1) * 128, t0:t0 + nt], KO[kb][:, :nt], R=[("KO", kb)])

                def evz(ps, pk, t0, nt, ti):
                    t1, k1 = tmp()
                    S.op("act", lambda e: e.activation(out=t1[:, :nt], in_=ps[:, :nt], func=AF.Tanh, scale=0.5), R=[pk], W=[k1])
                    S.op("dve", lambda e: e.scalar_tensor_tensor(out=SZ[:, t0:t0 + nt], in0=t1[:, :nt], scalar=1.0, in1=ps[:, :nt],
                                                                 op0=ALU.add, op1=ALU.mult), R=[k1, pk], W=[("SZ", sx, ti)])

                for (col0, ev) in ((0, evq), (128, evk)):
                    for ti in range(len(g.tiles)):
                        r_ = proj_mm(wb, wk, col0, g, ti)
                        yield "p"
                        ev(*r_)
                yield "qk"
                if is_s and n + 1 < NCH:
                    load_ctx(n + 1)
                units = []
                if is_s:
                    for r in range(LROWS):
                        rs = min(max(r - 4, 0), LROWS - 8)
                        vv, vn = (V0, "V0") if rs % 2 == 0 else (V1, "V1")
                        vb0 = rs // 2
                        units.append(dict(q0=r * GW, k0=rs * GW, nkl=512, tmo=(rs - r + 7) * GW, ctx=True,
                                          vv=vv, vn=vn, vblocks=[vb0 + x for x in range(4)]))
                else:
                    for p in range(2):
                        for qb in range(SEQ // 64):
                            units.append(dict(q0=p * SEQ + qb * 64, k0=p * SEQ, nkl=SEQ, tmo=None, ctx=False,
                                              vv=V0, vn="V0", vblocks=[2 * p, 2 * p + 1]))

                def emit_A(u, ui):
                    sp_ = PS[ui % 2]
                    sk = [("PS", ui % 2, 0), ("PS", ui % 2, 1)]
                    q0, k0, nkl = u["q0"], u["k0"], u["nkl"]
                    qti = g.tis(q0, 64)
                    kti = g.tis(k0, nkl)
                    hbs = [(0, 64, None), (64, 128, (64, 64))]
                    for (lo, hi, tp) in hbs:
                        S.op("pe", lambda e, lo=lo, hi=hi, tp=tp: e.matmul(
                            sp_[lo:hi, 0:nkl], lhsT=QT[lo:hi, q0:q0 + 64], rhs=KT[lo:hi, k0:k0 + nkl],
                            start=True, stop=(u["tmo"] is None), tile_position=tp),
                            R=[("QT", sx, x) for x in qti] + [("KT", sx, x) for x in kti], W=[sk[0]])
                    if u["tmo"] is not None:
                        for (lo, hi, tp) in hbs:
                            S.op("pe", lambda e, lo=lo, hi=hi, tp=tp: e.matmul(
                                sp_[lo:hi, 0:nkl], lhsT=IDB[lo:hi, lo:hi], rhs=TM[lo:hi, u["tmo"]:u["tmo"] + nkl],
                                start=False, stop=True, tile_position=tp), R=["IDB", TMk], W=[sk[0]])
                    if u["ctx"]:
                        for (lo, hi, tp) in hbs:
                            S.op("pe", lambda e, lo=lo, hi=hi, tp=tp: e.matmul(
                                sp_[lo:hi, 512:512 + PAST], lhsT=QT[lo:hi, q0:q0 + 64], rhs=KC[lo:hi, :],
                                start=True, stop=True, tile_position=tp),
                                R=[("QT", sx, x) for x in qti] + [KCk], W=[sk[1]])

                def _sm(u, ui):
                    sp_ = PS[ui % 2]
                    sk = [("PS", ui % 2, 0), ("PS", ui % 2, 1)]
                    nk = 1024 if u["ctx"] else u["nkl"]
                    rsk = sk if u["ctx"] else sk[:1]
                    return sp_, nk, rsk, STT[ui % 2], ("ST", ui % 2), PB_[ui % 3], ("P", ui % 3)

                def emit_B(u, ui):
                    sp_, nk, rsk, stt, stk, P_, pk_ = _sm(u, ui)
                    S.op("dve", lambda e: e.reduce_max(out=stt[:, 0:1], in_=sp_[:, 0:nk], axis=AX.X, negate=True), R=rsk, W=[stk])

                def emit_C(u, ui):
                    sp_, nk, rsk, stt, stk, P_, pk_ = _sm(u, ui)
                    S.op("act", lambda e: e.activation(out=P_[:, 0:nk], in_=sp_[:, 0:nk], func=AF.Exp, bias=stt[:, 0:1], scale=1.0,
                                                       accum_out=stt[:, 2:3]), R=rsk + [stk], W=[pk_, ("SM", ui % 2)])

                def emit_D(u, ui):
                    sp_, nk, rsk, stt, stk, P_, pk_ = _sm(u, ui)
                    smk = ("SM", ui % 2)
                    S.op("pool", lambda e: e.tensor_tensor(out=stt[:, 3:4], in0=stt[:, 2:3], in1=NEG1[:], op=ALU.pow),
                         R=[smk, "NEG1"], W=[("RS", ui % 2)])
                    S.op("pool", lambda e: e.tensor_tensor(out=DD[ui % 3][:], in0=IDB[:], in1=stt[:, 3:4].to_broadcast([128, 128]), op=ALU.mult),
                         R=["IDB", ("RS", ui % 2)], W=[("DD", ui % 3)])

                def emit_E(u, ui):
                    nk = 1024 if u["ctx"] else u["nkl"]
                    P_, pk_ = PB_[ui % 3], ("P", ui % 3)
                    for jj in range(nk // 128):
                        S.op("pe", lambda e, jj=jj: e.matmul(PS[2][:, jj * 128:(jj + 1) * 128], lhsT=P_[:, jj * 128:(jj + 1) * 128],
                                                              rhs=DD[ui % 3][:], start=True, stop=True),
                             R=[pk_, ("DD", ui % 3)], W=[("PS", 2, jj // 4)])

                def emit_F(u, ui):
                    nk = 1024 if u["ctx"] else u["nkl"]
                    PT_, ptk = PTB[ui % 2], ("PT", ui % 2)
                    h0 = min(nk, 512)
                    S.op("act", lambda e: e.activation(out=PT_[:, 0:h0], in_=PS[2][:, 0:h0], func=AF.Copy),
                         R=[("PS", 2, 0)], W=[(ptk, 0)])
                    if nk > 512:
                        S.op("dve", lambda e: e.tensor_copy(out=PT_[:, 512:nk], in_=PS[2][:, 512:nk]),
                             R=[("PS", 2, 1)], W=[(ptk, 1)])

                def emit_G(u, ui):
                    slot = ui % 8
                    bank = (ui // 8) % 2
                    ak = ("PS", 3, bank)
                    ap_ = PS[3][:, bank * 512:(bank + 1) * 512]
                    PT_, ptk = PTB[ui % 2], ("PT", ui % 2)
                    chunks = [(u["vv"], u["vn"], b, jj) for jj, b in enumerate(u["vblocks"])]
                    if u["ctx"]:
                        chunks += [(VC, None, b, 4 + b) for b in range(PAST // 128)]
                    for ci, (vv, vn, b, jj) in enumerate(chunks):
                        rk = [(vn, sx, b)] if vn is not None else [VCk]
                        for (lo, hi) in ((0, 64), (64, 128)):
                            S.op("pe", lambda e, vv=vv, b=b, jj=jj, lo=lo, hi=hi, ci=ci: e.matmul(
                                ap_[lo:hi, slot * 64:(slot + 1) * 64], lhsT=vv[:, b, lo:hi], rhs=PT_[:, jj * 128 + lo:jj * 128 + hi],
                                start=(ci == 0), stop=(ci == len(chunks) - 1), tile_position=((0, lo) if lo else None)),
                                R=rk + [(ptk, jj // 4)], W=[ak])
                    if slot == 7 or ui == nu - 1:
                        qs = units[ui - slot]["q0"]
                        nq = (slot + 1) * 64
                        S.op("dve", lambda e, qs=qs, nq=nq, ap_=ap_: e.tensor_tensor(
                            out=GT[:, n, qs:qs + nq], in0=ap_[:, 0:nq], in1=SZ[:, qs:qs + nq], op=ALU.mult),
                            R=[ak] + [("SZ", sx, x) for x in g.tis(qs, nq)], W=[("GT", n, x) for x in g.tis(qs, nq)])

                units = units[:DBG.get('units', len(units))]
                nu = len(units)

                def prologue():
                    emit_A(units[0], 0)
                    if nu > 1:
                        emit_A(units[1], 1)
                    emit_B(units[0], 0)
                    emit_C(units[0], 0)
                    emit_D(units[0], 0)

                if is_s:
                    prologue()
                    state["pool"] = [(2, 0), (2, 1), (3, 0), (3, 1)]
                for ti in range(len(g.tiles)):
                    r_ = proj_mm(wb, wk, 384, g, ti)
                    yield "p"
                    evz(*r_)

                for (VV, off, vname) in ((V0, 0, "V0"),):
                    blocks = []
                    b = 0
                    while off + 128 * b < T:
                        blocks.append((b, off + 128 * b, min(128, T - off - 128 * b)))
                        b += 1
                    for q0 in range(0, len(blocks), 4):
                        quad = blocks[q0:q0 + 4]
                        ps, pk = projps()
                        for qi, (b, t0, m) in enumerate(quad):
                            for c in range(NCH):
                                S.op("pe", lambda e, c=c, qi=qi, t0=t0, m=m: e.matmul(
                                    ps[:m, qi * 128:(qi + 1) * 128], lhsT=HTG[g.gi][:, c, t0:t0 + m], rhs=wb[:, c, 256:384],
                                    start=(c == 0), stop=(c == NCH - 1)),
                                    R=[wk] + [("HT", g.gi, c, x) for x in g.tis(t0, m)], W=[pk])
                        yield "p"
                        for qi, (b, t0, m) in enumerate(quad):
                            S.op("dve", lambda e, qi=qi, b=b, m=m: e.tensor_copy(out=VV[:m, b, :], in_=ps[:m, qi * 128:(qi + 1) * 128]),
                                 R=[pk], W=[(vname, sx, b)])
                            if not is_s and 'vo' not in DBG.get('skip', ''):
                                kb = kos["k"] % 2
                                kos["k"] += 1
                                S.op("act", lambda e, qi=qi, m=m, kb=kb: e.activation(out=KO[kb][:m, 0:128], in_=ps[:m, qi * 128:(qi + 1) * 128], func=AF.Copy),
                                     R=[pk], W=[("KO", kb)])
                                S.dma("sp", dq["ko%d" % kb], nv_d[j, t0:t0 + m, n * 128:(n + 1) * 128], KO[kb][:m, 0:128], R=[("KO", kb)])

                if is_s:
                    S.dma("sp", dq["v1"], V1[0:64, 0:NB, :], V0[64:128, 0:NB, :],
                          R=[("V0", 0, b) for b in range(NB)], W=[("V1", 0, b) for b in range(NB)])
                    S.dma("sp", dq["v1"], V1[64:128, 0:NB - 1, :], V0[0:64, 1:NB, :],
                          R=[("V0", 0, b) for b in range(NB)], W=[("V1", 0, b) for b in range(NB)])
                if is_s:
                    state["pool"] = PP_ALL
                yield "ready"
                if DBG.get('stage', 9) < 2:
                    return
                if not is_s:
                    prologue()
                    yield "primed"
                for k in range(nu + 3):
                    if k + 1 < nu:
                        emit_B(units[k + 1], k + 1)
                    if 0 <= k - 2 < nu:
                        emit_F(units[k - 2], k - 2)
                    if k + 1 < nu:
                        emit_C(units[k + 1], k + 1)
                        emit_D(units[k + 1], k + 1)
                    if 0 <= k - 3 < nu:
                        emit_G(units[k - 3], k - 3)
                    if k + 2 < nu:
                        emit_A(units[k + 2], k + 2)
                    if is_s or k < nu - 1:
                        filler(k, nu)
                    if 0 <= k - 1 < nu:
                        emit_E(units[k - 1], k - 1)
                    if not is_s and k >= nu - 1:
                        filler(k, nu)

            if NSET > 1:
                state["pool"] = [(0, 1), (1, 1), (2, 1)]
            projdone = [False] * npairs
            qkdone = [False] * npairs
            gens = []

            def advance(m, stop_at_qk=False):
                if m >= npairs or projdone[m] or (stop_at_qk and qkdone[m]):
                    return
                r = next(gens[m])
                if r == "qk":
                    qkdone[m] = True
                elif r == "ready":
                    projdone[m] = True

            primed = [False] * (npairs + 1)

            def make_filler(m):
                if NSET > 1:
                    def fp(k, nu):
                        if m >= npairs:
                            return
                        if not projdone[m]:
                            advance(m)
                        elif k >= nu - 1 and not primed[m]:
                            primed[m] = True
                            assert next(gens[m]) == "primed"
                    return fp

                def f(k, nu):
                    if k >= nu - 2:
                        advance(m, True)
                        advance(m, True)
                        advance(m, True)
                return f

            for n_ in range(npairs):
                gens.append(pair_body(n_, make_filler(n_ + 1)))
            for n_ in range(npairs):
                while not projdone[n_]:
                    advance(n_)
                for _ in gens[n_]:
                    pass
            state["pool"] = PP_ALL

            out_proj(g, i, "nwout", j, *hook())

        def final_norm_gen(g):
            Xg = X[g.gi]
            y_d = ysT_d if g.gi == 0 else ypT_d
            for ti, (t0, nt) in enumerate(g.tiles):
                t1, k1 = rms_rstd(g, t0, nt, lambda c: ("X", g.gi, c))
                for c in range(NCH):
                    t2, k2 = tmp()
                    S.op("dve", lambda e, c=c, t2=t2: e.scalar_tensor_tensor(
                        out=t2[:, :nt], in0=Xg[:, c, t0:t0 + nt], scalar=PV[:, 4, c:c + 1], in1=t1[:, :nt],
                        op0=ALU.mult, op1=ALU.mult), R=[("X", g.gi, c), "PVEC", k1], W=[k2])
                    S.dma("sp", tsem[k2[1]], y_d[c * 128:(c + 1) * 128, t0:t0 + nt], t2[:, :nt], R=[k2])
                    yield

        def final_norm(g):
            for _ in final_norm_gen(g):
                pass

        groups = [g_ for g_ in [GS, GP] if g_.name in DBG.get('groups', 'SP')]
        both = (len(groups) == 2)

        def run_layer(i, g, hook):
            ada_flush(i)
            with ExitStack() as lst:
                if i % 2 == 0:
                    conv_layer(i, g, lst, hook)
                else:
                    na_layer(i, g, lst, hook)
                S.barrier()

        if depth == 0:
            for g in groups:
                final_norm(g)
        elif not both:
            g = groups[0]
            for i in range(depth):
                modulate(g, i)
                run_layer(i, g, lambda: (None, 0))
            final_norm(g)
        else:
            modulate(GS, 0)
            for i in range(depth):
                run_layer(i, GS, lambda i=i: (modulate_gen(GP, i), 9 * len(GP.tiles)))

                def hook_p(i=i):
                    if i + 1 < depth:
                        ada_flush(i + 1)
                        return modulate_gen(GS, i + 1), 9 * len(GS.tiles)
                    return final_norm_gen(GS), 8 * len(GS.tiles)
                run_layer(i, GP, hook_p)
            final_norm(GP)
        S.barrier()
    return nc, WREC


_NAMES = ["x_prompt", "x_sample", "c", "cache_k", "cache_v", "c_ctx", "norm_g", "ada_w", "ada_b",
          "conv_w_in", "conv_dw_w", "conv_dw_b", "conv_ln_g", "conv_ln_b", "conv_w_out",
          "na_w_in", "na_rpb", "na_w_out", "final_g"]


def _fm(v):
    return np.ascontiguousarray(np.asarray(v, np.float32).reshape(NCH, 128).T)


def _bias_tables(na_rpb):
    qc = np.arange(GW)[:, None]
    kc = np.arange(GW)[None, :]
    qstart = np.clip(qc - 8, 0, GW - 16)
    valid = (kc >= qstart) & (kc < qstart + 16)
    dc = np.clip(kc - qc + 15, 0, 30)
    tm = np.empty((2, 8, 128, 15, GW), np.float32)
    for j in range(2):
        for h in range(16):
            g = na_rpb[j, h][:, dc]
            g = np.where(valid[None], g, np.float32(NEG)).transpose(1, 0, 2)
            tm[j, h // 2, (h % 2) * 64:(h % 2) * 64 + 64] = g
    return np.ascontiguousarray(tm.reshape(2, 8, 128, 15 * GW))


def make_in_maps(inp):
    a = {k: np.asarray(inp[k], dtype=np.float32) for k in _NAMES}
    pv_list = [a["norm_g"][i] for i in range(4)] + [a["final_g"]] + \
              [a["conv_dw_b"][j] for j in range(2)] + [a["conv_ln_g"][j] for j in range(2)] + \
              [a["conv_ln_b"][j] for j in range(2)]
    pv = np.ascontiguousarray(np.stack([_fm(v) for v in pv_list], axis=1))
    adab = np.ascontiguousarray(np.stack([a["ada_b"][i].reshape(24, 128).T for i in range(4)], axis=1))
    dww = np.ascontiguousarray(np.stack(
        [a["conv_dw_w"][j].T.reshape(NCH, 128, CK).transpose(1, 0, 2) for j in range(2)], axis=1))
    tm = _bias_tables(a["na_rpb"])
    ident = np.eye(128, dtype=np.float32)
    cwi = a["conv_w_in"]
    cab = np.ascontiguousarray(np.stack(
        [np.concatenate([cwi[j][:, n * 128:(n + 1) * 128], cwi[j][:, D + n * 128:D + (n + 1) * 128]], axis=1)
         for j in range(2) for n in range(8)], axis=0))
    cz = np.ascontiguousarray(cwi[:, :, 2 * D:3 * D])
    nwi = a["na_w_in"]
    nwp = np.ascontiguousarray(np.stack(
        [np.concatenate([nwi[j][:, part * D + n * 128:part * D + (n + 1) * 128] for part in range(4)], axis=1)
         for j in range(2) for n in range(8)], axis=0))
    dwwr = np.ascontiguousarray(np.roll(dww, -64, axis=0))
    shared = dict(pv=pv, adab=adab, dww=dww, dwwr=dwwr, tm=tm, ident=ident, ada_w=a["ada_w"], cab=cab, cz=cz,
                  conv_w_out=a["conv_w_out"], nwp=nwp, na_w_out=a["na_w_out"])
    maps = []
    for c in range(NCORES):
        b = c // 2
        g0 = 0 if c % 2 == 0 else (32 - LROWS)
        m = dict(shared)
        m["xsT"] = np.ascontiguousarray(a["x_sample"][b, g0 * GW:g0 * GW + TS].T)
        m["xpT"] = np.ascontiguousarray(a["x_prompt"][2 * c:2 * c + 2].reshape(TP, D).T)
        m["cvec"] = np.ascontiguousarray(np.stack([_fm(a["c"][b]), _fm(a["c_ctx"])], axis=2))
        m["ckT"] = np.ascontiguousarray(a["cache_k"][b].reshape(2, PAST, D).transpose(0, 2, 1))
        m["cv"] = np.ascontiguousarray(a["cache_v"][b].reshape(2, PAST, D))
        maps.append(m)
    return maps


def assemble(results):
    y_prompt = np.empty((16, SEQ, D), np.float32)
    y_sample = np.empty((4, 2048, D), np.float32)
    nk = np.empty((16, 2, SEQ, 16, 64), np.float32)
    nvv = np.empty((16, 2, SEQ, 16, 64), np.float32)
    for c in range(NCORES):
        r = results[c]
        b = c // 2
        ys = r["ysT"].T
        if c % 2 == 0:
            y_sample[b, 0:1024] = ys[0:1024]
        else:
            lo = (16 - (32 - LROWS)) * GW
            y_sample[b, 1024:2048] = ys[lo:lo + 1024]
        y_prompt[2 * c:2 * c + 2] = r["ypT"].T.reshape(2, SEQ, D)
        for jl in range(2):
            kt = r["nkT"][jl].T.reshape(2, SEQ, 16, 64)
            vt = r["nv"][jl].reshape(2, SEQ, 16, 64)
            nk[2 * c:2 * c + 2, jl] = kt
            nvv[2 * c:2 * c + 2, jl] = vt
    return y_prompt, y_sample, nk, nvv


_CACHE = {}


def kernel(**inputs):
    if "nc" not in _CACHE:
        _, rec = build_program()
        _CACHE["nc"] = build_program(sched=rec)[0]
    nc = _CACHE["nc"]
    in_maps = make_in_maps(inputs)
    res = run_bass_kernel_spmd(nc, in_maps, core_ids=list(range(NCORES)))
    return assemble(res.results)
```
